# Optimizing a Trainium2 kernel written in Bass

```python
import math
import jax, jax.numpy as jnp
from jax import lax
import numpy as np

D_MODEL = 2048
BATCH = 16
SEQ = 256
DEPTH = 1
DEC_BATCH = 4
DEC_SEQ = 2048
PAST_LEN = 256

GRID_W = 64
HEAD_DIM = 128
HYENA_WIDTH = D_MODEL // 2
ATTN_WIDTH = D_MODEL - HYENA_WIDTH
N_Q_HEADS = ATTN_WIDTH // HEAD_DIM
N_KV_HEADS = 2
GQA_GROUP = N_Q_HEADS // N_KV_HEADS
QKV_DIM = (N_Q_HEADS + 2 * N_KV_HEADS) * HEAD_DIM
IN_DIM = 3 * HYENA_WIDTH + QKV_DIM
D_FF = 4 * D_MODEL
HYENA_ORDER = 2
SHORT_CONV = 3
FILTER_EMB = 33
FILTER_BANDS = (FILTER_EMB - 1) // 2
FILTER_HIDDEN = 64
DECAY_TARGET = 1e-2
DECAY_FAST_PCT = 0.3
DECAY_SLOW_PCT = 1.5
MIN_DECAY = math.log(DECAY_TARGET) / DECAY_SLOW_PCT
MAX_DECAY = math.log(DECAY_TARGET) / DECAY_FAST_PCT
ROPE_THETA = 10000.0
Q_BLOCK = 128
LN_EPS = 1e-5
RMS_EPS = 1e-6
DEEPNORM_ALPHA = (2 * DEPTH) ** 0.25
DEEPNORM_BETA = (8 * DEPTH) ** -0.25

kernel_name = 'hyena_gqa_hybrid_dit_step'

F32 = jnp.float32


def layer_norm(x, g, b):
    xf = x.astype(F32)
    mu = jnp.mean(xf, axis=-1, keepdims=True)
    var = jnp.mean(jnp.square(xf - mu), axis=-1, keepdims=True)
    y = (xf - mu) * lax.rsqrt(var + LN_EPS)
    return (y * g.astype(F32) + b.astype(F32)).astype(x.dtype)


def rms_norm(x, g):
    xf = x.astype(F32)
    y = xf * lax.rsqrt(jnp.mean(jnp.square(xf), axis=-1, keepdims=True) + RMS_EPS)
    return (y * g.astype(F32)).astype(x.dtype)


def centred_short_conv(u, w, b):
    L = u.shape[1]
    up = jnp.pad(u, ((0, 0), (1, 1), (0, 0)))
    return up[:, :L] * w[0] + up[:, 1:L + 1] * w[1] + up[:, 2:] * w[2] + b


def hyena_filter_spectra(L, w1, b1, f1, w2, b2, f2, w3, b3):
    t = jnp.linspace(0.0, 1.0, L, dtype=F32)[:, None]
    w = 2.0 * math.pi * jnp.arange(L, dtype=F32)[:, None] / L
    bands = jnp.linspace(1e-4, FILTER_BANDS - 1, FILTER_BANDS, dtype=F32)[None, :]
    z = jnp.concatenate([t, jnp.cos(bands * w), -jnp.sin(bands * w)], axis=-1)
    h = jnp.sin(f1.astype(F32) * (z @ w1.astype(F32) + b1.astype(F32)))
    h = jnp.sin(f2.astype(F32) * (h @ w2.astype(F32) + b2.astype(F32)))
    h = h @ w3.astype(F32) + b3.astype(F32)
    h = h.reshape(L, HYENA_ORDER, 2, HYENA_WIDTH)
    deltas = jnp.abs(jnp.linspace(MIN_DECAY, MAX_DECAY, HYENA_WIDTH, dtype=F32))
    decay = jnp.exp(-t * deltas[None, :])
    h = h * decay[:, None, None, :]
    h_fwd, h_bwd = h[:, :, 0], h[:, :, 1]
    circ = jnp.concatenate([h_fwd, jnp.zeros_like(h_fwd[:1]), jnp.flip(h_bwd[1:], axis=0)], axis=0)
    return jnp.fft.rfft(circ, axis=0)


def long_conv(z, kf, bias):
    L = z.shape[1]
    zf = z.astype(F32)
    y = jnp.fft.irfft(jnp.fft.rfft(zf, n=2 * L, axis=1) * kf[None], n=2 * L, axis=1)[:, :L]
    return (y + zf * bias.astype(F32)).astype(z.dtype)


def hyena_mixer(u, conv_w, conv_b, fw1, fb1, ff1, fw2, fb2, ff2, fw3, fb3, filt_bias):
    L = u.shape[1]
    u = centred_short_conv(u, conv_w, conv_b)
    v, x1, x2 = jnp.split(u, 3, axis=-1)
    kf = hyena_filter_spectra(L, fw1, fb1, ff1, fw2, fb2, ff2, fw3, fb3)
    z = x1 * long_conv(v, kf[:, 0], filt_bias[0])
    return x2 * long_conv(z, kf[:, 1], filt_bias[1])


def axial_rope(L):
    rows = L // GRID_W
    row = jnp.broadcast_to(jnp.arange(rows, dtype=F32)[:, None], (rows, GRID_W)).reshape(-1)
    col = jnp.broadcast_to(jnp.arange(GRID_W, dtype=F32)[None, :], (rows, GRID_W)).reshape(-1)
    n = HEAD_DIM // 4
    inv = ROPE_THETA ** (-jnp.arange(n, dtype=F32) / n)
    ang = jnp.concatenate([row[:, None] * inv, col[:, None] * inv], axis=-1)
    return jnp.cos(ang), jnp.sin(ang)


def apply_rope(x, cos, sin):
    half = HEAD_DIM // 2
    x1 = x[..., :half].astype(F32)
    x2 = x[..., half:].astype(F32)
    c = cos[None, :, None, :]
    s = sin[None, :, None, :]
    return jnp.concatenate([x1 * c - x2 * s, x1 * s + x2 * c], axis=-1).astype(x.dtype)


def block_attention(q, k, v):
    B, Lq = q.shape[0], q.shape[1]
    nblk = Lq // Q_BLOCK
    qb = jnp.moveaxis(q.reshape(B, nblk, Q_BLOCK, N_KV_HEADS, GQA_GROUP, HEAD_DIM), 1, 0)
    scale = HEAD_DIM ** -0.5

    def one_block(q_blk):
        s = jnp.einsum('bqkgd,bskd->bkgqs', q_blk, k).astype(F32) * scale
        p = jax.nn.softmax(s, axis=-1).astype(v.dtype)
        return jnp.einsum('bkgqs,bskd->bqkgd', p, v)

    o = lax.map(one_block, qb)
    return jnp.moveaxis(o, 0, 1).reshape(B, Lq, N_Q_HEADS * HEAD_DIM)


def trunk_layer(x, mod, ctx_kv, rope, lp):
    (w_in, conv_w, conv_b, fw1, fb1, ff1, fw2, fb2, ff2, fw3, fb3, filt_bias,
     q_gain, k_gain, w_out, ln1_g, ln1_b, w_up, w_down, ln2_g, ln2_b) = lp
    B, L = x.shape[0], x.shape[1]
    shift1, scale1, gate1, shift2, scale2, gate2 = jnp.split(mod, 6, axis=-1)
    h = x * (1 + scale1) + shift1
    proj = h @ w_in
    o0 = 3 * HYENA_WIDTH
    o1 = o0 + N_Q_HEADS * HEAD_DIM
    o2 = o1 + N_KV_HEADS * HEAD_DIM
    hy = hyena_mixer(proj[..., :o0], conv_w, conv_b, fw1, fb1, ff1, fw2, fb2, ff2, fw3, fb3, filt_bias)
    q = rms_norm(proj[..., o0:o1].reshape(B, L, N_Q_HEADS, HEAD_DIM), q_gain)
    k = rms_norm(proj[..., o1:o2].reshape(B, L, N_KV_HEADS, HEAD_DIM), k_gain)
    v = proj[..., o2:].reshape(B, L, N_KV_HEADS, HEAD_DIM)
    new_kv = (k, v)
    if rope is not None:
        q = apply_rope(q, rope[0], rope[1])
        k = apply_rope(k, rope[0], rope[1])
    if ctx_kv is not None:
        k_all = jnp.concatenate([ctx_kv[0], k], axis=1)
        v_all = jnp.concatenate([ctx_kv[1], v], axis=1)
    else:
        k_all, v_all = k, v
    attn = block_attention(q, k_all, v_all)
    mix = jnp.concatenate([hy, attn], axis=-1) @ w_out
    x = layer_norm(DEEPNORM_ALPHA * x + gate1 * mix, ln1_g, ln1_b)
    h = x * (1 + scale2) + shift2
    ff = jnp.square(jax.nn.relu(h @ w_up)) @ w_down
    x = layer_norm(DEEPNORM_ALPHA * x + gate2 * ff, ln2_g, ln2_b)
    return x, new_kv


def setup_inputs(seed: int = 0) -> dict:
    key = jax.random.key(seed)
    ks = jax.random.split(key, 32)
    nrm = jax.random.normal
    D, HY = D_MODEL, HYENA_WIDTH
    return {
        'x_prompt': nrm(ks[0], (BATCH, SEQ, D), F32),
        'x_sample': nrm(ks[1], (DEC_BATCH, DEC_SEQ, D), F32),
        'cache_k': nrm(ks[2], (DEC_BATCH, DEPTH, PAST_LEN, N_KV_HEADS, HEAD_DIM), F32),
        'cache_v': nrm(ks[3], (DEC_BATCH, DEPTH, PAST_LEN, N_KV_HEADS, HEAD_DIM), F32),
        'c': nrm(ks[4], (DEC_BATCH, D), F32),
        'c_ctx': nrm(ks[5], (D,), F32),
        'w_mod': nrm(ks[6], (DEPTH, D, 6 * D), F32) * (0.5 * D ** -0.5),
        'b_mod': nrm(ks[7], (DEPTH, 6 * D), F32) * 0.01,
        'w_in': nrm(ks[8], (DEPTH, D, IN_DIM), F32) * D ** -0.5,
        'conv_w': nrm(ks[9], (DEPTH, SHORT_CONV, 3 * HY), F32) * SHORT_CONV ** -0.5,
        'conv_b': nrm(ks[10], (DEPTH, 3 * HY), F32) * 0.01,
        'filt_w1': nrm(ks[11], (DEPTH, FILTER_EMB, FILTER_HIDDEN), F32) * FILTER_EMB ** -0.5,
        'filt_b1': nrm(ks[12], (DEPTH, FILTER_HIDDEN), F32) * 0.1,
        'filt_freq1': 1.0 + 0.1 * nrm(ks[13], (DEPTH, FILTER_HIDDEN), F32),
        'filt_w2': nrm(ks[14], (DEPTH, FILTER_HIDDEN, FILTER_HIDDEN), F32) * FILTER_HIDDEN ** -0.5,
        'filt_b2': nrm(ks[15], (DEPTH, FILTER_HIDDEN), F32) * 0.1,
        'filt_freq2': 1.0 + 0.1 * nrm(ks[16], (DEPTH, FILTER_HIDDEN), F32),
        'filt_w3': nrm(ks[17], (DEPTH, FILTER_HIDDEN, HYENA_ORDER * 2 * HY), F32) * 0.02,
        'filt_b3': nrm(ks[18], (DEPTH, HYENA_ORDER * 2 * HY), F32) * 0.01,
        'filt_bias': nrm(ks[19], (DEPTH, HYENA_ORDER, HY), F32),
        'q_gain': 1.0 + 0.01 * nrm(ks[20], (DEPTH, HEAD_DIM), F32),
        'k_gain': 1.0 + 0.01 * nrm(ks[21], (DEPTH, HEAD_DIM), F32),
        'w_out': nrm(ks[22], (DEPTH, D, D), F32) * (DEEPNORM_BETA * D ** -0.5),
        'ln1_g': 1.0 + 0.01 * nrm(ks[23], (DEPTH, D), F32),
        'ln1_b': 0.01 * nrm(ks[24], (DEPTH, D), F32),
        'w_up': nrm(ks[25], (DEPTH, D, D_FF), F32) * D ** -0.5,
        'w_down': nrm(ks[26], (DEPTH, D_FF, D), F32) * (DEEPNORM_BETA * D_FF ** -0.5),
        'ln2_g': 1.0 + 0.01 * nrm(ks[27], (DEPTH, D), F32),
        'ln2_b': 0.01 * nrm(ks[28], (DEPTH, D), F32),
    }


def reference(x_prompt, x_sample, cache_k, cache_v, c, c_ctx, w_mod, b_mod, w_in, conv_w, conv_b,
              filt_w1, filt_b1, filt_freq1, filt_w2, filt_b2, filt_freq2, filt_w3, filt_b3, filt_bias,
              q_gain, k_gain, w_out, ln1_g, ln1_b, w_up, w_down, ln2_g, ln2_b):
    rope = axial_rope(x_sample.shape[1])
    xp, xs = x_prompt, x_sample
    new_k, new_v = [], []
    for l in range(DEPTH):
        lp = (w_in[l], conv_w[l], conv_b[l], filt_w1[l], filt_b1[l], filt_freq1[l], filt_w2[l],
              filt_b2[l], filt_freq2[l], filt_w3[l], filt_b3[l], filt_bias[l], q_gain[l], k_gain[l],
              w_out[l], ln1_g[l], ln1_b[l], w_up[l], w_down[l], ln2_g[l], ln2_b[l])
        mod_ctx = (jax.nn.silu(c_ctx)[None] @ w_mod[l] + b_mod[l])[:, None, :]
        mod_lat = (jax.nn.silu(c) @ w_mod[l] + b_mod[l])[:, None, :]
        xp, kv = trunk_layer(xp, mod_ctx, None, None, lp)
        new_k.append(kv[0])
        new_v.append(kv[1])
        xs, _ = trunk_layer(xs, mod_lat, (cache_k[:, l], cache_v[:, l]), rope, lp)
    new_cache_k = jnp.stack(new_k, axis=1)
    new_cache_v = jnp.stack(new_v, axis=1)
    return (xp, xs, new_cache_k, new_cache_v)
```

```python
import numpy as np
import concourse.bass as bass
import concourse.mybir as mybir

F32 = mybir.dt.float32
BF16 = mybir.dt.bfloat16
AF = mybir.ActivationFunctionType
ALU = mybir.AluOpType
AX = mybir.AxisListType

EPOCH = 8000
NDMASEM = 12
SAME_ENGINE_SYNC = True


class Buf:
    __slots__ = ("lw", "rd", "name", "excl")

    def __init__(self, name="", excl=False):
        self.lw = None
        self.rd = {}
        self.name = name
        self.excl = excl


class Op:
    __slots__ = ("eng", "fn", "deps", "needs_inc", "tok", "dma", "idx", "pre", "solo")

    def __init__(self, eng, fn, dma):
        self.eng = eng
        self.fn = fn
        self.dma = dma
        self.deps = []
        self.needs_inc = dma
        self.tok = None
        self.pre = None
        self.solo = False


class _Rec:
    def __init__(self):
        self.call = None

    def __getattr__(self, name):
        def f(*a, **k):
            self.call = (name, a, k)
            return self
        return f


class Prog:
    ENGS = ("pe", "act", "dve", "pool", "sp")

    def __init__(self, nc):
        self.nc = nc
        self.ops = {e: [] for e in self.ENGS}
        self.final_waits = []
        self.pending = {e: [] for e in self.ENGS}
        self.phase = "init"
        self.pe_phase = []

    def barrier(self):
        deps = []
        for e in self.ENGS:
            last = None
            dmas = []
            for o in reversed(self.ops[e]):
                if o.solo:
                    continue
                if o.dma:
                    if len(dmas) < NDMASEM:
                        dmas.append(o)
                elif last is None:
                    last = o
                if last is not None and len(dmas) >= NDMASEM:
                    break
            if last is not None:
                last.needs_inc = True
                deps.append(last)
            deps.extend(dmas)
        for e in self.ENGS:
            self.pending[e] = list(deps)

    def add(self, eng, fn, reads=(), writes=(), dma=False, solo=False):
        rec = _Rec()
        fn(rec)
        dma = dma or solo
        op = Op(eng, rec.call, dma)
        op.solo = solo
        if eng == "pe":
            self.pe_phase.append(self.phase)
        lst = self.ops[eng]
        op.idx = len(lst)
        deps = {}

        def dep(o, war=False):
            if o is None:
                return
            if o.eng == eng and not o.dma and not dma:
                if eng == "pe" or not SAME_ENGINE_SYNC:
                    return
            deps[id(o)] = o

        for b in reads:
            dep(b.lw)
            if b.excl:
                for o in b.rd.values():
                    if o.eng != eng:
                        dep(o)
        for b in writes:
            dep(b.lw)
            for o in b.rd.values():
                dep(o, war=True)
        for b in reads:
            b.rd[eng if not dma else (eng, "dma", op.idx)] = op
        for b in writes:
            b.lw = op
            b.rd = {}
        if self.pending[eng]:
            for o in self.pending[eng]:
                if not (o.eng == eng and not o.dma and not dma):
                    deps[id(o)] = o
            self.pending[eng] = []
        op.deps = list(deps.values())
        for o in op.deps:
            o.needs_inc = True
        lst.append(op)
        return op

    def emit(self, extra_ctx=()):
        import contextlib

        nc = self.nc
        with contextlib.ExitStack() as st:
            sems = {}
            for e in self.ENGS:
                n_inc = sum(1 for o in self.ops[e] if o.needs_inc and not o.dma)
                n_ep = max(1, (n_inc + EPOCH - 1) // EPOCH)
                sems[e] = [st.enter_context(nc.semaphore(f"s_{e}_{i}")) for i in range(n_ep)]
                n_dma = sum(1 for o in self.ops[e] if o.dma and not o.solo)
                dsem = [st.enter_context(nc.semaphore(f"d_{e}_{i}")) for i in range(min(NDMASEM, n_dma))]
                ci = 0
                di = 0
                for o in self.ops[e]:
                    if o.solo:
                        o.tok = (st.enter_context(nc.semaphore(f"solo_{e}_{o.idx}")), 1)
                    elif o.dma:
                        s = dsem[di % NDMASEM]
                        o.tok = (s, 16 * (di // NDMASEM + 1))
                        if di >= NDMASEM:
                            o.pre = (s, 16 * (di // NDMASEM))
                        di += 1
                    elif o.needs_inc:
                        o.tok = (sems[e][ci // EPOCH], ci % EPOCH + 1)
                        ci += 1
            block = st.enter_context(nc.Block())
            ops = self.ops
            final_waits = self.final_waits

            def gen(e, eng):
                seen = {}

                def wait(tok):
                    s, v = tok
                    k = id(s)
                    if seen.get(k, 0) >= v:
                        return
                    seen[k] = v
                    eng.wait_ge(s, v)

                for o in ops[e]:
                    if o.pre is not None:
                        wait(o.pre)
                    for d in o.deps:
                        wait(d.tok)
                    name, a, k = o.fn
                    ins = getattr(eng, name)(*a, **k)
                    if o.needs_inc:
                        ins.then_inc(o.tok[0], 1 if o.solo else (16 if o.dma else 1))
                if e == "sp":
                    for o in final_waits:
                        wait(o.tok)

            @block.tensor
            def _(eng):
                gen("pe", eng)

            @block.scalar
            def _(eng):
                gen("act", eng)

            @block.vector
            def _(eng):
                gen("dve", eng)

            @block.gpsimd
            def _(eng):
                gen("pool", eng)

            @block.sync
            def _(eng):
                gen("sp", eng)

import math
import contextlib
from concourse.bass_utils import run_bass_kernel_spmd
import ml_dtypes

D = 2048
HY = 1024
HD = 128
O0, O1, O2 = 3072, 4096, 4352
IN_DIM = 4608
DFF = 8192
LN_EPS = 1e-5
RMS_EPS = 1e-6
ALPHA = 2.0 ** 0.25
USE_SHARED_KF = False
MAGIC = 12582912.0
TWO_PI = 2.0 * math.pi


def _host_consts(L):
    N = 2 * L
    t = np.arange(L, dtype=np.float64)
    f = np.arange(L, dtype=np.float64)
    ang = np.pi * np.outer(t, 2 * f + 1) / N
    FC = np.cos(ang)
    FS = -np.sin(ang)
    nt = L // 128
    def slab(M):
        return M.reshape(nt, 128, nt, 128).transpose(2, 1, 0, 3)
    FCS = np.stack([slab(FC), slab(FS)], axis=2)
    IC = (2.0 / N) * FC.T
    IS = (2.0 / N) * FS.T
    ICS = np.stack([IC.reshape(nt, 128, L), IS.reshape(nt, 128, L)], axis=2)
    tl = np.linspace(0.0, 1.0, L, dtype=np.float32)
    w = (2.0 * np.float32(math.pi) * np.arange(L, dtype=np.float32) / np.float32(L)).astype(np.float32)
    bands = np.linspace(1e-4, 15.0, 16, dtype=np.float32)
    arg = (bands[None, :] * w[:, None]).astype(np.float32).astype(np.float64)
    z = np.concatenate([tl[:, None].astype(np.float64), np.cos(arg), -np.sin(arg)], axis=1)
    zT = np.ascontiguousarray(z.T).astype(np.float32)
    min_decay = math.log(1e-2) / 1.5
    max_decay = math.log(1e-2) / 0.3
    deltas = np.abs(np.linspace(min_decay, max_decay, HY, dtype=np.float32)).astype(np.float64)
    dec = np.exp(-tl.astype(np.float64)[:, None] * deltas[None, :]).astype(np.float32)
    dec0 = dec.copy()
    dec0[0, :] = 0.0
    tcol = np.ascontiguousarray((-tl).reshape(nt, 128).T).astype(np.float32)
    return dict(
        tcol=tcol, deltas=deltas.astype(np.float32).reshape(1, HY),
        FCS=np.ascontiguousarray(FCS).astype(ml_dtypes.bfloat16),
        ICS=np.ascontiguousarray(ICS).astype(ml_dtypes.bfloat16),
        zT=zT, dec=dec, dec0=dec0,
    )


def _rope_tables(L):
    rows = L // 64
    row = np.repeat(np.arange(rows, dtype=np.float32), 64)
    col = np.tile(np.arange(64, dtype=np.float32), rows)
    n = HD // 4
    inv = (10000.0 ** (-np.arange(n, dtype=np.float32) / n)).astype(np.float32)
    ang = np.concatenate([row[:, None] * inv, col[:, None] * inv], axis=-1).astype(np.float32)
    c = np.cos(ang.astype(np.float64)).astype(np.float32)
    s = np.sin(ang.astype(np.float64)).astype(np.float32)
    C2 = np.concatenate([c, c], axis=1).T
    S2 = np.concatenate([s, s], axis=1).T
    return np.ascontiguousarray(C2), np.ascontiguousarray(S2)


def build_program(stage=99):
    nc = bass.Bass("TRN2", target_bir_lowering=False)

    def din(name, shape, dt=F32):
        return nc.dram_tensor(name, list(shape), dt, kind="ExternalInput").ap()

    def dout(name, shape):
        return nc.dram_tensor(name, list(shape), F32, kind="ExternalOutput").ap()

    def dscr(name, shape, dt=F32):
        return nc.dram_tensor(name, list(shape), dt, kind="Internal").ap()

    x_s = din("x_s", [2048, D])
    x_p = din("x_p", [512, D])
    cache_k = din("cache_k", [256, 256])
    cache_v = din("cache_v", [256, 256])
    c2 = din("c2", [2, D])
    w_mod = din("w_mod", [D, 6 * D])
    b_mod = din("b_mod", [1, 6 * D])
    w_in = din("w_in", [D, IN_DIM])
    conv_w = din("conv_w", [3, 3072])
    conv_b = din("conv_b", [3072])
    fw1 = din("fw1", [33, 64]); fb1 = din("fb1", [64, 1]); ff1 = din("ff1", [64, 1])
    fw2 = din("fw2", [64, 64]); fb2 = din("fb2", [64, 1]); ff2 = din("ff2", [64, 1])
    fw3 = din("fw3", [64, 4096]); fb3 = din("fb3", [1, 4096])
    filt_bias = din("filt_bias", [2, HY])
    q_gain = din("q_gain", [1, 128]); k_gain = din("k_gain", [1, 128])
    w_out = din("w_out", [D, D])
    ln1_g = din("ln1_g", [1, D]); ln1_b = din("ln1_b", [1, D])
    w_up = din("w_up", [D, DFF]); w_down = din("w_down", [DFF, D])
    ln2_g = din("ln2_g", [1, D]); ln2_b = din("ln2_b", [1, D])
    ident_d = din("ident", [128, 128]); rt_d = din("rt", [128, 128]); sig_d = din("sig", [128, 1])
    c2tab = din("c2tab", [128, 2048]); s2tab = din("s2tab", [128, 2048])
    CS = {}
    for nm, L in (("s", 2048), ("p", 256)):
        nt = L // 128
        CS[nm] = dict(
            FCS=din(f"FCS_{nm}", [nt, 128, 2, nt, 128], BF16),
            ICS=din(f"ICS_{nm}", [nt, 128, 2, L], BF16),
            zT=din(f"zT_{nm}", [33, L]),
            tcol=din(f"tcol_{nm}", [128, nt]),
            K=dscr(f"Kscr_{nm}", [2, 2, nt, 128, HY], BF16),
        )
    modscr = dscr("modscr", [2, 6 * D])
    deltas_d = din("deltas", [1, HY])
    if USE_SHARED_KF:
        fw3s = din("fw3s", [64, 512]); fb3s = din("fb3s", [1, 512])
        decs = din("decs", [2048, 128]); dec0s = din("dec0s", [2048, 128])
    else:
        fw3s = fb3s = decs = dec0s = None
    Kpart = nc.dram_tensor("Kpart", [16, 128, 2, 256], F32, kind="Internal")
    Kall = nc.dram_tensor("Kall", [8, 16, 128, 2, 256], F32, kind="Internal")
    B_Kpart = Buf(); B_Kall = Buf()
    y_s = dout("y_s", [1024, D])
    y_p = dout("y_p", [512, D])
    nk_p = dout("nk_p", [512, 256])
    nv_p = dout("nv_p", [512, 256])
    dbg = dout("dbg", [128, 256])

    P = Prog(nc)
    out_ops = []
    uid = [0]

    def sb(st, name, shape, dt=F32):
        uid[0] += 1
        return st.enter_context(nc.sbuf_tensor(f"{name}_{uid[0]}", list(shape), dt))

    def dma(q, out, in_, reads=(), writes=(), **kw):
        return P.add(q, lambda e: e.dma_start(out=out, in_=in_, **kw), reads=reads, writes=writes, dma=True)

    @contextlib.contextmanager
    def scope():
        with contextlib.ExitStack() as st_:
            try:
                yield st_
            finally:
                P.barrier()

    class Slots:
        def __init__(self, st, name, shape, dt, n):
            self.t = [sb(st, f"{name}{i}", shape, dt) for i in range(n)]
            self.b = [Buf(f"{name}{i}") for i in range(n)]
            self.i = 0

        def next(self):
            k = self.i % len(self.t)
            self.i += 1
            return self.t[k], self.b[k]

    with contextlib.ExitStack() as top:
        banks = [top.enter_context(nc.psum_tensor(f"bank{i}", [128, 512], F32)) for i in range(8)]
        bbuf = [Buf(f"bank{i}", excl=True) for i in range(8)]
        banks_bf = [bk_.bitcast(BF16) for bk_ in banks]
        pc = {"A": 0, "B": 0}

        def psA():
            k = pc["A"] % 4
            pc["A"] += 1
            return banks[k], bbuf[k]

        def psB():
            k = 4 + pc["B"] % 4
            pc["B"] += 1
            return banks[k], bbuf[k]

        ident = sb(top, "ident", [128, 128]); B_id = Buf()
        ones_f = sb(top, "ones_f", [128, 128]); ones_b = sb(top, "ones_b", [128, 128], BF16); B_ones = Buf()
        RT = sb(top, "RT", [128, 128]); B_rt = Buf()
        sig = sb(top, "sig", [128, 1]); B_sig = Buf()
        epsr = sb(top, "epsr", [128, 1]); epsl = sb(top, "epsl", [128, 1]); B_eps = Buf()
        pcol = sb(top, "pcol", [128, 128]); B_pcol = Buf()
        mcol = sb(top, "mcol", [128, 128]); B_mcol = Buf()
        B_modscr = Buf()
        dma("sp", ident[:], ident_d, writes=[B_id])
        dma("sp", RT[:], rt_d, writes=[B_rt])
        dma("sp", sig[:], sig_d, writes=[B_sig])
        P.add("pool", lambda e: e.memset(ones_f[:], 1.0), writes=[B_ones])
        P.add("pool", lambda e: e.memset(ones_b[:], 1.0), writes=[B_ones])
        P.add("pool", lambda e: e.memset(epsr[:], RMS_EPS), writes=[B_eps])
        P.add("pool", lambda e: e.memset(epsl[:], LN_EPS), writes=[B_eps])

        with scope() as st:
            prow = sb(st, "prow", [128, 128]); B_prow = Buf()
            P.add("pool", lambda e: e.memset(prow[:], 0.0), writes=[B_prow])
            dma("sp", prow[0:72, :], conv_w.rearrange("k (c p) -> (k c) p", p=128), writes=[B_prow])
            dma("sp", prow[72:96, :], conv_b.rearrange("(c p) -> c p", p=128), writes=[B_prow])
            dma("sp", prow[96:112, :], filt_bias.rearrange("o (c p) -> (o c) p", p=128), writes=[B_prow])
            dma("sp", prow[112:113, :], q_gain, writes=[B_prow])
            dma("sp", prow[113:114, :], k_gain, writes=[B_prow])
            bk, bb = psA()
            P.add("pe", lambda e: e.transpose(bk[:, 0:128], prow[:], ident[:]), reads=[B_prow, B_id], writes=[bb])
            P.add("dve", lambda e: e.tensor_copy(out=pcol[:], in_=bk[:, 0:128]), reads=[bb], writes=[B_pcol])

        def mod_setup(st):
            cT = sb(st, "cT", [128, 2, 16]); sT = sb(st, "sT", [128, 2, 16], BF16); B_cT = Buf(); B_sT = Buf()
            for r in range(2):
                dma("sp", cT[:, r, :], c2[r].rearrange("(kc p) -> p kc", p=128), writes=[B_cT],
                    allow_slow_non_contiguous=True)
            P.add("act", lambda e: e.activation(out=sT[:], in_=cT[:], func=AF.Silu), reads=[B_cT], writes=[B_sT])
            wm = Slots(st, "wm", [128, 16, 512], BF16, 2)
            bm = Slots(st, "bm", [2, 512], F32, 2)
            mr = Slots(st, "mr", [2, 512], F32, 2)
            return mod_g(sT, B_sT, wm, bm, mr)

        def mod_g(sT, B_sT, wm, bm, mr):
            for cc in range(24):
                wt, wb = wm.next()
                dma("pool", wt[:], w_mod[:, cc * 512:(cc + 1) * 512].rearrange("(kc p) n -> p kc n", p=128), writes=[wb])
                bt, btb = bm.next()
                dma("sp", bt[:], b_mod[:, cc * 512:(cc + 1) * 512].partition_broadcast(2), writes=[btb])
                bk, bb = psA()
                for kc in range(16):
                    P.add("pe", lambda e, bk=bk, wt=wt, kc=kc: e.matmul(bk[0:2, :], sT[:, :, kc], wt[:, kc, :],
                                                                         start=(kc == 0), stop=(kc == 15)),
                          reads=[B_sT, wb], writes=[bb])
                mt, mb = mr.next()
                P.add("dve", lambda e, mt=mt, bk=bk, bt=bt: e.tensor_tensor(out=mt[:], in0=bk[0:2, :], in1=bt[:], op=ALU.add),
                      reads=[bb, btb], writes=[mb])
                dma("act", modscr[:, cc * 512:(cc + 1) * 512], mt[:], reads=[mb], writes=[B_modscr])
                yield

        def mod_finish(st):
            prow2 = sb(st, "prow2", [128, 128]); B_prow2 = Buf()
            for r in range(2):
                for ki, kind in enumerate((0, 1, 3, 4)):
                    dma("sp", prow2[(r * 4 + ki) * 16:(r * 4 + ki + 1) * 16, :],
                        modscr[r, kind * D:(kind + 1) * D].rearrange("(c p) -> c p", p=128),
                        reads=[B_modscr], writes=[B_prow2])
            bk, bb = psA()
            P.add("pe", lambda e, bk=bk: e.transpose(bk[:, 0:128], prow2[:], ident[:]), reads=[B_prow2, B_id], writes=[bb])
            P.add("dve", lambda e, bk=bk: e.tensor_copy(out=mcol[:], in_=bk[:, 0:128]), reads=[bb], writes=[B_mcol])
            for r in range(2):
                a = (r * 4 + 1) * 16
                P.add("dve", lambda e, a=a: e.tensor_scalar(out=mcol[:, a:a + 16], in0=mcol[:, a:a + 16], scalar1=1.0,
                                                            scalar2=None, op0=ALU.add), reads=[B_mcol], writes=[B_mcol])
                a = (r * 4 + 3) * 16
                P.add("dve", lambda e, a=a: e.tensor_scalar(out=mcol[:, a:a + 16], in0=mcol[:, a:a + 16], scalar1=1.0,
                                                            scalar2=1.0 / ALPHA, op0=ALU.add, op1=ALU.mult),
                      reads=[B_mcol], writes=[B_mcol])

        def MC(r, ki, chunk):
            a = (r * 4 + ki) * 16 + chunk
            return mcol[:, a:a + 1]

        def PCW(k, chunk):
            return pcol[:, k * 24 + chunk:k * 24 + chunk + 1]

        def PCB(chunk):
            return pcol[:, 72 + chunk:73 + chunk]

        def PFB(o, ch8):
            return pcol[:, 96 + o * 8 + ch8:97 + o * 8 + ch8]

        QG = pcol[:, 112:113]
        KG = pcol[:, 113:114]

        def phase_kf(nm, L, shared=False, extra=None):
            P.phase = "kf_" + nm
            cs = CS[nm]
            nt = L // 128
            with scope() as st:
                w1 = sb(st, "w1", [33, 64]); w2 = sb(st, "w2", [64, 64]); w3 = sb(st, "w3", [65, 512 if shared else 4096], BF16)
                pf = sb(st, "pf", [64, 8]); B_w = Buf()
                zT = sb(st, "zT", [33, L])
                dma("sp", w1[:], fw1, writes=[B_w]); dma("sp", w2[:], fw2, writes=[B_w])
                dma("pool", w3[0:64, :], fw3s if shared else fw3, writes=[B_w]); dma("pool", w3[64:65, :], fb3s if shared else fb3, writes=[B_w])
                dma("sp", zT[:], cs["zT"], writes=[B_w])
                for i, a in enumerate((fb1, ff1, fb2, ff2)):
                    dma("sp", pf[:, i:i + 1], a, writes=[B_w])
                P.add("dve", lambda e: e.tensor_tensor(out=pf[:, 4:5], in0=pf[:, 0:1], in1=pf[:, 1:2], op=ALU.mult), reads=[B_w], writes=[B_w])
                P.add("dve", lambda e: e.tensor_tensor(out=pf[:, 5:6], in0=pf[:, 2:3], in1=pf[:, 3:4], op=ALU.mult), reads=[B_w], writes=[B_w])
                h1 = sb(st, "h1", [64, L]); h2 = sb(st, "h2", [65, L], BF16); B_h1 = Buf(); B_h2 = Buf()
                arg = sb(st, "arg", [64, 512]); kk = sb(st, "kk", [64, 512]); B_arg = Buf()
                P.add("pool", lambda e: e.memset(h2[64:65, :], 1.0), writes=[B_h2])

                def sin_layer(wt, src, B_src, krows, fcol, fbcol, dst, B_dst):
                    for t0 in range(0, L, 512):
                        n = min(512, L - t0)
                        bk, bb = psA()
                        P.add("pe", lambda e, bk=bk, t0=t0, n=n: e.matmul(bk[0:64, 0:n], wt[0:krows, :], src[0:krows, t0:t0 + n], start=True, stop=True),
                              reads=[B_w, B_src], writes=[bb])
                        P.add("act", lambda e, bk=bk, n=n: e.activation(out=arg[:, 0:n], in_=bk[0:64, 0:n], func=AF.Identity, scale=fcol, bias=fbcol),
                              reads=[bb, B_w], writes=[B_arg])
                        P.add("dve", lambda e, n=n: e.tensor_scalar(out=kk[:, 0:n], in0=arg[:, 0:n], scalar1=1.0 / TWO_PI, scalar2=MAGIC, op0=ALU.mult, op1=ALU.add),
                              reads=[B_arg], writes=[B_arg])
                        P.add("dve", lambda e, n=n: e.tensor_scalar(out=kk[:, 0:n], in0=kk[:, 0:n], scalar1=MAGIC, scalar2=-TWO_PI, op0=ALU.subtract, op1=ALU.mult),
                              reads=[B_arg], writes=[B_arg])
                        P.add("dve", lambda e, n=n: e.tensor_tensor(out=arg[:, 0:n], in0=arg[:, 0:n], in1=kk[:, 0:n], op=ALU.add),
                              reads=[B_arg], writes=[B_arg])
                        P.add("act", lambda e, t0=t0, n=n: e.activation(out=dst[0:64, t0:t0 + n], in_=arg[:, 0:n], func=AF.Sin),
                              reads=[B_arg], writes=[B_dst])

                sin_layer(w1, zT, B_w, 33, pf[:, 1:2], pf[:, 4:5], h1, B_h1)
                sin_layer(w2, h1, B_h1, 64, pf[:, 3:4], pf[:, 5:6], h2, B_h2)
                if shared:
                    hs = sb(st, "hs", [128, nt, 256], BF16); hd = sb(st, "hd", [128, nt, 256], BF16); B_hs = Buf(); B_hd = Buf()
                    dcs = Slots(st, "dcs", [128, 2, 128], F32, 2)
                    tmp = Slots(st, "ktmp", [128, 2, 128], F32, 2)
                    slab = Slots(st, "kslab", [128, 2, nt, 128], BF16, 3)
                    ko = Slots(st, "ko", [128, 2, 256], F32, 2)
                    for tc in range(nt):
                        dt_, db = dcs.next()
                        dma("sp", dt_[:, 0, :], decs[tc * 128:(tc + 1) * 128, :], writes=[db])
                        dma("sp", dt_[:, 1, :], dec0s[tc * 128:(tc + 1) * 128, :], writes=[db])
                        bk, bb = psA()
                        P.add("pe", lambda e, bk=bk, tc=tc: e.matmul(bk[:, :], h2[0:65, tc * 128:(tc + 1) * 128], w3[0:65, :], start=True, stop=True),
                              reads=[B_h2, B_w], writes=[bb])
                        for o in range(2):
                            tt_, tb = tmp.next()
                            for d_ in range(2):
                                ca = (o * 2 + d_) * 128
                                P.add("dve", lambda e, bk=bk, tt_=tt_, dt_=dt_, d_=d_, ca=ca: e.tensor_tensor(out=tt_[:, d_, :], in0=bk[:, ca:ca + 128], in1=dt_[:, d_, :], op=ALU.mult),
                                      reads=[bb, db], writes=[tb])
                            P.add("dve", lambda e, tt_=tt_, tc=tc, o=o: e.tensor_tensor(out=hs[:, tc, o * 128:(o + 1) * 128], in0=tt_[:, 0, :], in1=tt_[:, 1, :], op=ALU.add),
                                  reads=[tb], writes=[B_hs])
                            P.add("dve", lambda e, tt_=tt_, tc=tc, o=o: e.tensor_tensor(out=hd[:, tc, o * 128:(o + 1) * 128], in0=tt_[:, 0, :], in1=tt_[:, 1, :], op=ALU.subtract),
                                  reads=[tb], writes=[B_hd])
                    for fc in range(nt):
                        sl, slb = slab.next()
                        dma("sp", sl[:], cs["FCS"][fc], writes=[slb])
                        b1, bb1 = psA()
                        b2, bb2 = psA()
                        for tc in range(nt):
                            P.add("pe", lambda e, b1=b1, sl=sl, tc=tc: e.matmul(b1[:, 0:256], sl[:, 0, tc, :], hs[:, tc, :], start=(tc == 0), stop=(tc == nt - 1)),
                                  reads=[slb, B_hs], writes=[bb1])
                        for tc in range(nt):
                            P.add("pe", lambda e, b2=b2, sl=sl, tc=tc: e.matmul(b2[:, 0:256], sl[:, 1, tc, :], hd[:, tc, :], start=(tc == 0), stop=(tc == nt - 1)),
                                  reads=[slb, B_hd], writes=[bb2])
                        kt, kb = ko.next()
                        P.add("act", lambda e, kt=kt, b1=b1: e.activation(out=kt[:, 0, :], in_=b1[:, 0:256], func=AF.Copy), reads=[bb1], writes=[kb])
                        P.add("act", lambda e, kt=kt, b2=b2: e.activation(out=kt[:, 1, :], in_=b2[:, 0:256], func=AF.Copy), reads=[bb2], writes=[kb])
                        dma("act", Kpart.ap()[fc], kt[:], reads=[kb], writes=[B_Kpart])
                    P.add("pool", lambda e: e.collective_compute("AllGather", ALU.bypass, replica_groups=[list(range(8))], ins=[Kpart.ap().opt()], outs=[Kall.ap().opt()]),
                          reads=[B_Kpart], writes=[B_Kall], solo=True)
                    return
                dlt = sb(st, "dlt", [128, HY]); tcl = sb(st, "tcl", [128, nt]); B_dlt = Buf()
                dma("sp", dlt[:], deltas_d.partition_broadcast(128), writes=[B_dlt])
                dma("sp", tcl[:], cs["tcol"], writes=[B_dlt])
                hsS = Slots(st, "hs", [128, nt, 512], BF16, 2); hdS = Slots(st, "hd", [128, nt, 512], BF16, 2)
                dcs = Slots(st, "dcs", [128, 1, 512], F32, 2)
                tmp = Slots(st, "ktmp", [128, 2, 512], F32, 2)
                slab = Slots(st, "kslab", [128, 2, nt, 128], BF16, 2)
                ko = Slots(st, "ko", [128, 2, 512], BF16, 2)
                def gen_g(o, cb, hs, B_hs, hd, B_hd):
                    c0 = cb * 512
                    for tc in range(nt):
                        dt_, db = dcs.next()
                        P.add("act", lambda e, dt_=dt_, tc=tc: e.activation(out=dt_[:, 0, :], in_=dlt[:, c0:c0 + 512], func=AF.Exp, scale=tcl[:, tc:tc + 1]),
                              reads=[B_dlt], writes=[db])
                        tt_, tb = tmp.next()
                        for d_ in range(2):
                            bk, bb = psA()
                            col = o * 2048 + d_ * 1024 + c0
                            P.add("pe", lambda e, bk=bk, tc=tc, col=col: e.matmul(bk[:, :], h2[0:65, tc * 128:(tc + 1) * 128], w3[0:65, col:col + 512], start=True, stop=True),
                                  reads=[B_h2, B_w], writes=[bb])
                            P.add("dve", lambda e, bk=bk, tt_=tt_, dt_=dt_, d_=d_: e.tensor_tensor(out=tt_[:, d_, :], in0=bk[:, :], in1=dt_[:, 0, :], op=ALU.mult),
                                  reads=[bb, db], writes=[tb])
                        if tc == 0:
                            P.add("dve", lambda e, tt_=tt_: e.memset(tt_[0:1, 1, :], 0.0), reads=[tb], writes=[tb])
                        P.add("dve", lambda e, tt_=tt_, tc=tc, hs=hs: e.tensor_tensor(out=hs[:, tc, :], in0=tt_[:, 0, :], in1=tt_[:, 1, :], op=ALU.add),
                              reads=[tb], writes=[B_hs])
                        P.add("dve", lambda e, tt_=tt_, tc=tc, hd=hd: e.tensor_tensor(out=hd[:, tc, :], in0=tt_[:, 0, :], in1=tt_[:, 1, :], op=ALU.subtract),
                              reads=[tb], writes=[B_hd])
                        yield

                def dft_g(o, cb, hs, B_hs, hd, B_hd):
                    c0 = cb * 512
                    for fc in range(nt):
                        sl, slb = slab.next()
                        dma("sp", sl[:], cs["FCS"][fc], writes=[slb])
                        b1, bb1 = psB()
                        b2, bb2 = psB()
                        for tc in range(nt):
                            P.add("pe", lambda e, b1=b1, sl=sl, tc=tc: e.matmul(b1[:, :], sl[:, 0, tc, :], hs[:, tc, :], start=(tc == 0), stop=(tc == nt - 1)),
                                  reads=[slb, B_hs], writes=[bb1])
                        for tc in range(nt):
                            P.add("pe", lambda e, b2=b2, sl=sl, tc=tc: e.matmul(b2[:, :], sl[:, 1, tc, :], hd[:, tc, :], start=(tc == 0), stop=(tc == nt - 1)),
                                  reads=[slb, B_hd], writes=[bb2])
                        kt, kb = ko.next()
                        P.add("act", lambda e, kt=kt, b1=b1: e.activation(out=kt[:, 0, :], in_=b1[:, :], func=AF.Copy), reads=[bb1], writes=[kb])
                        P.add("act", lambda e, kt=kt, b2=b2: e.activation(out=kt[:, 1, :], in_=b2[:, :], func=AF.Copy, scale=sig[:, 0:1]), reads=[bb2, B_sig], writes=[kb])
                        dma("act", cs["K"][o, :, fc, :, c0:c0 + 512].rearrange("r p c -> p r c"), kt[:], reads=[kb], writes=[cs["KB"]])
                        yield

                rounds = [0]

                def ilv(*gens):
                    gens = list(gens)
                    while gens:
                        for g in list(gens):
                            try:
                                next(g)
                            except StopIteration:
                                gens.remove(g)
                        rounds[0] += 1
                        if extra is not None and rounds[0] % 3 == 0:
                            try:
                                next(extra)
                            except StopIteration:
                                pass
                prev = None
                for o in range(2):
                    for cb in range(2):
                        hs, B_hs = hsS.next()
                        hd, B_hd = hdS.next()
                        cur = (o, cb, hs, B_hs, hd, B_hd)
                        if prev is None:
                            ilv(gen_g(*cur))
                        else:
                            ilv(dft_g(*prev), gen_g(*cur))
                        prev = cur
                ilv(dft_g(*prev))

        CS["s"]["KB"] = Buf("Ks")
        CS["p"]["KB"] = Buf("Kp")

        def segment(nm, nb, L, To, x_ap, row, has_cache, rope, y_out, nk_out, nv_out, upto=3):
            cs = CS[nm]
            P.phase = nm + "_hT"
            T = nb * L
            nt = L // 128
            koff = 256 if has_cache else 0
            Lk = koff + L
            nkc = Lk // 128
            TL = min(512, L)
            with scope() as sseg:
                mix = sb(sseg, "mix", [128, 16, To], BF16)
                mixb = [[Buf() for _ in range(To // 128)] for _ in range(16)]

                def mixbufs(c, t0, n):
                    return [mixb[c][i] for i in range(t0 // 128, (t0 + n + 127) // 128)]

                with scope() as smx:
                    hT = sb(smx, "hT", [128, 16, T], BF16)
                    hTb = [[Buf() for _ in range(T // 128)] for _ in range(16)]

                    def hbufs(c, t0, n):
                        return [hTb[c][i] for i in range(t0 // 128, (t0 + n + 127) // 128)]

                    with scope() as st:
                        xt = Slots(st, "xt", [128, D], F32, 2)
                        for tt in range(T // 128):
                            xtile, xb = xt.next()
                            dma("sp", xtile[:], x_ap[tt * 128:(tt + 1) * 128, :], writes=[xb])
                            for g4 in range(4):
                                bk, bb = psA()
                                for j in range(4):
                                    c = g4 * 4 + j
                                    P.add("pe", lambda e, bk=bk, j=j, c=c, xtile=xtile: e.transpose(bk[:, j * 128:(j + 1) * 128], xtile[:, c * 128:(c + 1) * 128], ident[:]),
                                          reads=[xb, B_id], writes=[bb])
                                for j in range(4):
                                    c = g4 * 4 + j
                                    if g4 % 2 == 0:
                                        P.add("act", lambda e, bk=bk, j=j, c=c, tt=tt: e.activation(out=hT[:, c, tt * 128:(tt + 1) * 128], in_=bk[:, j * 128:(j + 1) * 128], func=AF.Identity,
                                                                                                    scale=MC(row, 1, c), bias=MC(row, 0, c)),
                                              reads=[bb, B_mcol], writes=[hTb[c][tt]])
                                    else:
                                        P.add("dve", lambda e, bk=bk, j=j, c=c, tt=tt: e.tensor_scalar(out=hT[:, c, tt * 128:(tt + 1) * 128], in0=bk[:, j * 128:(j + 1) * 128],
                                                                                                       scalar1=MC(row, 1, c), scalar2=MC(row, 0, c), op0=ALU.mult, op1=ALU.add),
                                              reads=[bb, B_mcol], writes=[hTb[c][tt]])

                    SUB = 9
                    if SUB < 2:
                        return
                    wsl = Slots(smx, "wsl", [128, 16, 128], BF16, 3)

                    def proj_fm(col, tiles, cb_):
                        for _ in proj_fm_g(col, tiles, cb_):
                            pass

                    def proj_fm_g(col, tiles, cb_):
                        wt, wb = wsl.next()
                        dma("pool", wt[:], w_in[:, col:col + 128].rearrange("(kc p) n -> p kc n", p=128), writes=[wb])
                        for (t0, n) in tiles:
                            bk, bb = psA()
                            for kc in range(16):
                                P.add("pe", lambda e, bk=bk, wt=wt, kc=kc, t0=t0, n=n: e.matmul(bk[:, 0:n], wt[:, kc, :], hT[:, kc, t0:t0 + n], start=(kc == 0), stop=(kc == 15)),
                                      reads=[wb] + hbufs(kc, t0, n), writes=[bb])
                            g_ = cb_(t0, n, bk, bb)
                            if g_ is not None:
                                for pg in list(pend):
                                    try:
                                        next(pg)
                                    except StopIteration:
                                        pend.remove(pg)
                                pend.append(g_)
                            yield

                    pend = []

                    def flush_pend():
                        while pend:
                            for pg in list(pend):
                                try:
                                    next(pg)
                                except StopIteration:
                                    pend.remove(pg)

                    all_tiles = [(b * L + t0, TL) for b in range(nb) for t0 in range(0, L, TL)]
                    own_tiles = [(t0, min(512, To - t0)) for t0 in range(0, To, 512)]

                    P.phase = nm + "_attnproj"
                    with scope() as st:
                        qT = sb(st, "qT", [128, 8, To], BF16); qTb = [Buf() for _ in range(8)]
                        kT = sb(st, "kT", [128, 2, nb, Lk], BF16); kTb = [[Buf() for _ in range(nb)] for _ in range(2)]
                        V = sb(st, "V", [128, nb, nkc, 256], BF16); Vb = [Buf() for _ in range(nb)]
                        if rope:
                            C2 = sb(st, "C2", [128, 2048]); S2 = sb(st, "S2", [128, 2048]); B_rope = Buf()
                            dma("sp", C2[:], c2tab, writes=[B_rope]); dma("sp", S2[:], s2tab, writes=[B_rope])
                        sqS = Slots(st, "sq", [128, 512], F32, 2); rsS = Slots(st, "rs", [128, 512], F32, 2)
                        knS = Slots(st, "kn", [128, 512], F32, 2); t2S = Slots(st, "t2", [128, 512], F32, 2)

                        def post(gcol, out_ap, out_bufs, pos0):
                            def f(t0, n, bk, bb):
                                sq, B_sq = sqS.next(); rs, B_rs = rsS.next(); kn, B_kn = knS.next(); t2, B_t2 = t2S.next()
                                P.add("act", lambda e: e.activation(out=sq[:, 0:n], in_=bk[:, 0:n], func=AF.Square), reads=[bb], writes=[B_sq])
                                b2, bb2 = psB()
                                P.add("pe", lambda e: e.matmul(b2[:, 0:n], ones_f[:], sq[:, 0:n], start=True, stop=True), reads=[B_ones, B_sq], writes=[bb2])
                                P.add("act", lambda e: e.activation(out=rs[:, 0:n], in_=b2[:, 0:n], func=AF.Sqrt, scale=1.0 / HD, bias=epsr[:, 0:1]), reads=[bb2, B_eps], writes=[B_rs])
                                P.add("dve", lambda e: e.reciprocal(out=rs[:, 0:n], in_=rs[:, 0:n]), reads=[B_rs], writes=[B_rs])
                                oa = out_ap(t0, n)
                                ob = out_bufs(t0, n)
                                if not rope:
                                    P.add("dve", lambda e: e.scalar_tensor_tensor(out=oa, in0=bk[:, 0:n], scalar=gcol, in1=rs[:, 0:n], op0=ALU.mult, op1=ALU.mult),
                                          reads=[bb, B_rs, B_pcol], writes=ob)
                                    return
                                P.add("dve", lambda e: e.scalar_tensor_tensor(out=kn[:, 0:n], in0=bk[:, 0:n], scalar=gcol, in1=rs[:, 0:n], op0=ALU.mult, op1=ALU.mult),
                                      reads=[bb, B_rs, B_pcol], writes=[B_kn])
                                yield
                                b3, bb3 = psB()
                                P.add("pe", lambda e: e.matmul(b3[:, 0:n], RT[:], kn[:, 0:n], start=True, stop=True), reads=[B_rt, B_kn], writes=[bb3])
                                p0 = pos0(t0)
                                P.add("dve", lambda e: e.tensor_tensor(out=t2[:, 0:n], in0=b3[:, 0:n], in1=S2[:, p0:p0 + n], op=ALU.mult), reads=[bb3, B_rope], writes=[B_t2])
                                P.add("dve", lambda e: e.tensor_tensor(out=kn[:, 0:n], in0=kn[:, 0:n], in1=C2[:, p0:p0 + n], op=ALU.mult), reads=[B_kn, B_rope], writes=[B_kn])
                                P.add("dve", lambda e: e.tensor_tensor(out=oa, in0=kn[:, 0:n], in1=t2[:, 0:n], op=ALU.add), reads=[B_kn, B_t2], writes=ob)
                                yield
                            return f

                        for hd_ in range(8):
                            proj_fm(O0 + hd_ * 128, own_tiles,
                                    post(QG, lambda t0, n, hd_=hd_: qT[:, hd_, t0:t0 + n], lambda t0, n, hd_=hd_: [qTb[hd_]], lambda t0: t0))
                        for kvh in range(2):
                            proj_fm(O1 + kvh * 128, all_tiles,
                                    post(KG, lambda t0, n, kvh=kvh: kT[:, kvh, t0 // L, koff + (t0 % L):koff + (t0 % L) + n],
                                         lambda t0, n, kvh=kvh: [kTb[kvh][t0 // L]], lambda t0: t0 % L))
                        if SUB < 3:
                            return
                        flush_pend()
                        wv = sb(st, "wv", [128, 16, 256], BF16); B_wv = Buf()
                        dma("pool", wv[:], w_in[:, O2:O2 + 256].rearrange("(kc p) n -> p kc n", p=128), writes=[B_wv])
                        if nv_out is not None:
                            wk = sb(st, "wk", [128, 16, 256], BF16); B_wk = Buf()
                            dma("pool", wk[:], w_in[:, O1:O1 + 256].rearrange("(kc p) n -> p kc n", p=128), writes=[B_wk])
                            kgb = sb(st, "kgb", [128, 128]); B_kgb = Buf()
                            dma("sp", kgb[:], k_gain.partition_broadcast(128), writes=[B_kgb])
                            vo = Slots(st, "vo", [128, 256], F32, 2)
                            kout = Slots(st, "kout", [128, 256], F32, 2)
                            sq2 = sb(st, "sq2", [128, 256]); ss = sb(st, "ss", [128, 2]); B_sq2 = Buf(); B_ss = Buf()
                        for tt in range(T // 128):
                            b_ = (tt * 128) // L
                            tcl = (tt * 128 % L) // 128
                            bk, bb = psA()
                            for kc in range(16):
                                P.add("pe", lambda e, bk=bk, kc=kc, tt=tt: e.matmul(bk[:, 0:256], hT[:, kc, tt * 128:(tt + 1) * 128], wv[:, kc, :], start=(kc == 0), stop=(kc == 15)),
                                      reads=[B_wv, hTb[kc][tt]], writes=[bb])
                            P.add("act", lambda e, bk=bk, b_=b_, tcl=tcl: e.activation(out=V[:, b_, koff // 128 + tcl, :], in_=bk[:, 0:256], func=AF.Copy), reads=[bb], writes=[Vb[b_]])
                            if nv_out is not None:
                                vt, vb = vo.next()
                                P.add("act", lambda e, bk=bk, vt=vt: e.activation(out=vt[:], in_=bk[:, 0:256], func=AF.Copy), reads=[bb], writes=[vb])
                                out_ops.append(dma("sp", nv_out[tt * 128:(tt + 1) * 128, :], vt[:], reads=[vb]))
                                bk2, bb2 = psA()
                                for kc in range(16):
                                    P.add("pe", lambda e, bk2=bk2, kc=kc, tt=tt: e.matmul(bk2[:, 0:256], hT[:, kc, tt * 128:(tt + 1) * 128], wk[:, kc, :], start=(kc == 0), stop=(kc == 15)),
                                          reads=[B_wk, hTb[kc][tt]], writes=[bb2])
                                P.add("act", lambda e, bk2=bk2: e.activation(out=sq2[:], in_=bk2[:, 0:256], func=AF.Square), reads=[bb2], writes=[B_sq2])
                                P.add("dve", lambda e: e.reduce_sum(out=ss[:], in_=sq2[:].rearrange("p (h d) -> p h d", h=2), axis=AX.X), reads=[B_sq2], writes=[B_ss])
                                P.add("act", lambda e: e.activation(out=ss[:], in_=ss[:], func=AF.Sqrt, scale=1.0 / HD, bias=epsr[:, 0:1]), reads=[B_ss, B_eps], writes=[B_ss])
                                P.add("dve", lambda e: e.reciprocal(out=ss[:], in_=ss[:]), reads=[B_ss], writes=[B_ss])
                                kt_, kb_ = kout.next()
                                for h in range(2):
                                    P.add("dve", lambda e, bk2=bk2, kt_=kt_, h=h: e.scalar_tensor_tensor(out=kt_[:, h * 128:(h + 1) * 128], in0=bk2[:, h * 128:(h + 1) * 128], scalar=ss[:, h:h + 1],
                                                                                                        in1=kgb[:], op0=ALU.mult, op1=ALU.mult),
                                          reads=[bb2, B_ss, B_kgb], writes=[kb_])
                                out_ops.append(dma("sp", nk_out[tt * 128:(tt + 1) * 128, :], kt_[:], reads=[kb_]))
                        if has_cache:
                            ck = sb(st, "ck", [128, 2, 256]); B_ck = Buf()
                            dma("sp", ck[:], cache_k.rearrange("(c p) f -> p c f", p=128), writes=[B_ck])
                            dma("pool", V[:, 0, 0:2, :], cache_v.rearrange("(c p) f -> p c f", p=128), writes=[Vb[0]])
                            bk, bb = psA()
                            for c in range(2):
                                for kvh in range(2):
                                    j = c * 2 + kvh
                                    P.add("pe", lambda e, bk=bk, j=j, c=c, kvh=kvh: e.transpose(bk[:, j * 128:(j + 1) * 128], ck[:, c, kvh * 128:(kvh + 1) * 128], ident[:]),
                                          reads=[B_ck, B_id], writes=[bb])
                            for c in range(2):
                                for kvh in range(2):
                                    j = c * 2 + kvh
                                    P.add("act", lambda e, bk=bk, j=j, c=c, kvh=kvh: e.activation(out=kT[:, kvh, 0, c * 128:(c + 1) * 128], in_=bk[:, j * 128:(j + 1) * 128], func=AF.Copy),
                                          reads=[bb], writes=[kTb[kvh][0]])
                        if SUB < 4:
                            return
                        P.phase = nm + "_attncore"
                        PT = Slots(st, "PT", [128, 512], BF16, 4)
                        rc = sb(st, "rc", [128, 512]); B_rc = Buf()
                        qn = To // nb
                        for b_ in range(nb):
                            for hd_ in range(8):
                                kvh = hd_ // 4
                                for q0 in range(0, qn, 512):
                                    n = min(512, qn - q0)
                                    qc = b_ * qn + q0
                                    po, pob = psB()
                                    pm, pmb = psB()
                                    Sd = {}

                                    def issue_S(kc):
                                        bk, bb = psA()
                                        P.add("pe", lambda e: e.matmul(bk[:, 0:n], kT[:, kvh, b_, kc * 128:(kc + 1) * 128], qT[:, hd_, qc:qc + n], start=True, stop=True),
                                              reads=[kTb[kvh][b_], qTb[hd_]], writes=[bb])
                                        pt, ptb = PT.next()
                                        P.add("act", lambda e: e.activation(out=pt[:, 0:n], in_=bk[:, 0:n], func=AF.Exp, scale=HD ** -0.5), reads=[bb], writes=[ptb])
                                        Sd[kc] = (pt, ptb)
                                    for kc in range(min(2, nkc)):
                                        issue_S(kc)
                                    for kc in range(nkc):
                                        if kc + 2 < nkc:
                                            issue_S(kc + 2)
                                        pt, ptb = Sd.pop(kc)
                                        P.add("pe", lambda e: e.matmul(po[:, 0:n], V[:, b_, kc, kvh * 128:(kvh + 1) * 128], pt[:, 0:n], start=(kc == 0), stop=(kc == nkc - 1)),
                                              reads=[Vb[b_], ptb], writes=[pob])
                                        P.add("pe", lambda e: e.matmul(pm[:, 0:n], ones_b[:], pt[:, 0:n], start=(kc == 0), stop=(kc == nkc - 1)),
                                              reads=[B_ones, ptb], writes=[pmb])
                                    P.add("dve", lambda e: e.reciprocal(out=rc[:, 0:n], in_=pm[:, 0:n]), reads=[pmb], writes=[B_rc])
                                    P.add("dve", lambda e: e.tensor_tensor(out=mix[:, 8 + hd_, qc:qc + n], in0=po[:, 0:n], in1=rc[:, 0:n], op=ALU.mult),
                                          reads=[pob, B_rc], writes=mixbufs(8 + hd_, qc, n))

                    with scope() as st:
                        if upto < 2:
                            return
                        P.phase = nm + "_hyena"
                        ncol = nb * 256
                        praw = sb(st, "praw", [128, nb, L + 2]); B_praw = Buf()
                        utmp = sb(st, "utmp", [128, nb, L]); B_ut = Buf()
                        vT = sb(st, "vT", [128, 2, T], BF16); vTb = [Buf(), Buf()]
                        xT = sb(st, "xT", [128, 2, T], BF16); xTb = [Buf(), Buf()]
                        vtok = sb(st, "vtok", [128, nt, ncol], BF16); B_vtok = Buf()
                        Yr = sb(st, "Yr", [128, nt, ncol], BF16); Yi = sb(st, "Yi", [128, nt, ncol], BF16); B_Y = Buf()
                        fsl = Slots(st, "fsl", [128, 2, nt, 128], BF16, 2)
                        isl = Slots(st, "isl", [128, 2, L], BF16, 2)
                        kcs = Slots(st, "kcs", [128, 2, 256], BF16 if not USE_SHARED_KF else F32, 2)
                        ctmp = Slots(st, "ctmp", [128, 4, 256], F32, 1)
                        P.add("pool", lambda e: e.memset(praw[:], 0.0), writes=[B_praw])

                        def proj_conv(colbase, blk, dst, dstb):
                            for _ in proj_conv_g(colbase, blk, dst, dstb):
                                pass

                        def proj_conv_g(colbase, blk, dst, dstb, tn=None):
                            tn = L if tn is None else tn
                            tiles_ = [(b_ * L + t0, min(TL, tn - t0)) for b_ in range(nb) for t0 in range(0, tn, TL)]
                            if tn < L:
                                tiles_ += [(b_ * L + tn, 128) for b_ in range(nb)]
                            for cj in range(2):
                                ch = (colbase // 128) + blk * 2 + cj

                                def cb_(t0, n, bk, bb):
                                    b_ = t0 // L
                                    tl = t0 % L
                                    P.add("act", lambda e: e.activation(out=praw[:, b_, 1 + tl:1 + tl + n], in_=bk[:, 0:n], func=AF.Copy), reads=[bb], writes=[B_praw])
                                yield from proj_fm_g(colbase + blk * 256 + cj * 128, tiles_, cb_)
                                P.add("dve", lambda e, ch=ch: e.tensor_scalar(out=utmp[:, :, 0:tn], in0=praw[:, :, 1:tn + 1], scalar1=PCW(1, ch), scalar2=PCB(ch), op0=ALU.mult, op1=ALU.add),
                                      reads=[B_praw, B_pcol], writes=[B_ut])
                                P.add("dve", lambda e, ch=ch: e.scalar_tensor_tensor(out=utmp[:, :, 0:tn], in0=praw[:, :, 0:tn], scalar=PCW(0, ch), in1=utmp[:, :, 0:tn], op0=ALU.mult, op1=ALU.add),
                                      reads=[B_praw, B_pcol, B_ut], writes=[B_ut])
                                P.add("dve", lambda e, ch=ch, cj=cj: e.scalar_tensor_tensor(out=dst[:, cj, :].rearrange("p (b l) -> p b l", b=nb)[:, :, 0:tn], in0=praw[:, :, 2:tn + 2], scalar=PCW(2, ch), in1=utmp[:, :, 0:tn],
                                                                                            op0=ALU.mult, op1=ALU.add),
                                      reads=[B_praw, B_pcol, B_ut], writes=[dstb[cj]])
                                yield

                        def interleave(*gens):
                            gens = list(gens)
                            while gens:
                                for g in list(gens):
                                    try:
                                        next(g)
                                    except StopIteration:
                                        gens.remove(g)

                        def to_tok(src, srcb):
                            for tc in range(nt):
                                bk, bb = psA()
                                for b_ in range(nb):
                                    for cj in range(2):
                                        j = b_ * 2 + cj
                                        P.add("pe", lambda e, bk=bk, j=j, b_=b_, cj=cj, tc=tc: e.transpose(banks_bf[bbuf.index(bb)][:, j * 128:(j + 1) * 128],
                                                                                                         src[:, cj, b_ * L + tc * 128:b_ * L + (tc + 1) * 128], identb[:]),
                                              reads=[srcb[cj], B_id], writes=[bb])
                                P.add("act", lambda e, bk=bk, tc=tc: e.activation(out=vtok[:, tc, :], in_=banks_bf[bbuf.index(bb)][:, 0:ncol], func=AF.Copy), reads=[bb], writes=[B_vtok])

                        def fwd_g(o, blk):
                            for fc in range(nt):
                                sl, slb = fsl.next()
                                dma("sp", sl[:], cs["FCS"][fc], writes=[slb])
                                kt, kb = kcs.next()
                                if USE_SHARED_KF and nm == "s":
                                    dma("sp", kt[:].rearrange("p i (r c) -> p i r c", r=2),
                                        Kall.ap()[2 * blk:2 * blk + 2, fc, :, :, o * 128:(o + 1) * 128].rearrange("r p i c -> p i r c"), reads=[B_Kall], writes=[kb])
                                    P.add("act", lambda e, kt=kt: e.activation(out=kt[:, 1, :], in_=kt[:, 1, :], func=AF.Copy, scale=sig[:, 0:1]), reads=[kb, B_sig], writes=[kb])
                                else:
                                    dma("sp", kt[:], cs["K"][o, :, fc, :, blk * 256:(blk + 1) * 256].rearrange("r p c -> p r c"), reads=[cs["KB"]], writes=[kb])
                                zr, zrb = psB()
                                zi, zib = psB()
                                for tc in range(nt):
                                    P.add("pe", lambda e, zr=zr, sl=sl, tc=tc: e.matmul(zr[:, 0:ncol], sl[:, 0, tc, :], vtok[:, tc, :], start=(tc == 0), stop=(tc == nt - 1)),
                                          reads=[slb, B_vtok], writes=[zrb])
                                for tc in range(nt):
                                    P.add("pe", lambda e, zi=zi, sl=sl, tc=tc: e.matmul(zi[:, 0:ncol], sl[:, 1, tc, :], vtok[:, tc, :], start=(tc == 0), stop=(tc == nt - 1)),
                                          reads=[slb, B_vtok], writes=[zib])
                                for b_ in range(nb):
                                    ct, ctb = ctmp.next()
                                    cs_ = slice(b_ * 256, (b_ + 1) * 256)
                                    P.add("dve", lambda e, ct=ct, zr=zr, kt=kt, cs_=cs_: e.tensor_tensor(out=ct[:, 0, :], in0=zr[:, cs_], in1=kt[:, 0, :], op=ALU.mult), reads=[zrb, kb], writes=[ctb])
                                    P.add("dve", lambda e, ct=ct, zi=zi, kt=kt, cs_=cs_: e.tensor_tensor(out=ct[:, 1, :], in0=zi[:, cs_], in1=kt[:, 1, :], op=ALU.mult), reads=[zib, kb], writes=[ctb])
                                    P.add("dve", lambda e, ct=ct, zr=zr, kt=kt, cs_=cs_: e.tensor_tensor(out=ct[:, 2, :], in0=zr[:, cs_], in1=kt[:, 1, :], op=ALU.mult), reads=[zrb, kb], writes=[ctb])
                                    P.add("dve", lambda e, ct=ct, zi=zi, kt=kt, cs_=cs_: e.tensor_tensor(out=ct[:, 3, :], in0=zi[:, cs_], in1=kt[:, 0, :], op=ALU.mult), reads=[zib, kb], writes=[ctb])
                                    P.add("dve", lambda e, ct=ct, fc=fc, cs_=cs_: e.tensor_tensor(out=Yr[:, fc, cs_], in0=ct[:, 0, :], in1=ct[:, 1, :], op=ALU.subtract), reads=[ctb], writes=[B_Y])
                                    P.add("dve", lambda e, ct=ct, fc=fc, cs_=cs_: e.tensor_tensor(out=Yi[:, fc, cs_], in0=ct[:, 2, :], in1=ct[:, 3, :], op=ALU.add), reads=[ctb], writes=[B_Y])
                                yield

                        def inv(tn, consume):
                            tiles = [(b_, t0, min(512, tn - t0)) for b_ in range(nb) for t0 in range(0, tn, 512)]
                            outs = [(cj, b_, t0, n) for cj in range(2) for (b_, t0, n) in tiles]
                            assert len(outs) <= 8
                            for fc in range(nt):
                                sl, slb = isl.next()
                                dma("sp", sl[:, :, 0:tn], cs["ICS"][fc, :, :, 0:tn], writes=[slb])
                                for i, (cj, b_, t0, n) in enumerate(outs):
                                    ca = b_ * 256 + cj * 128
                                    P.add("pe", lambda e, i=i, sl=sl, fc=fc, ca=ca, t0=t0, n=n: e.matmul(banks[i][:, 0:n], Yr[:, fc, ca:ca + 128], sl[:, 0, t0:t0 + n], start=(fc == 0), stop=False),
                                          reads=[B_Y, slb], writes=[bbuf[i]])
                                    P.add("pe", lambda e, i=i, sl=sl, fc=fc, ca=ca, t0=t0, n=n: e.matmul(banks[i][:, 0:n], Yi[:, fc, ca:ca + 128], sl[:, 1, t0:t0 + n], start=False, stop=(fc == nt - 1)),
                                          reads=[B_Y, slb], writes=[bbuf[i]])
                            for i, (cj, b_, t0, n) in enumerate(outs):
                                consume(cj, b_, t0, n, banks[i], bbuf[i])

                        qn = To // nb
                        for blk in range(4):
                            proj_conv(0, blk, vT, vTb)
                            to_tok(vT, vTb)
                            interleave(fwd_g(0, blk), proj_conv_g(1024, blk, xT, xTb))

                            def gate1(cj, b_, t0, n, bk, bb, blk=blk):
                                ch8 = blk * 2 + cj
                                a = b_ * L + t0
                                P.add("dve", lambda e: e.scalar_tensor_tensor(out=vT[:, cj, a:a + n], in0=vT[:, cj, a:a + n], scalar=PFB(0, ch8), in1=bk[:, 0:n], op0=ALU.mult, op1=ALU.add),
                                      reads=[vTb[cj], bb, B_pcol], writes=[vTb[cj]])
                                P.add("dve", lambda e: e.tensor_tensor(out=vT[:, cj, a:a + n], in0=vT[:, cj, a:a + n], in1=xT[:, cj, a:a + n], op=ALU.mult),
                                      reads=[vTb[cj], xTb[cj]], writes=[vTb[cj]])
                            inv(L, gate1)
                            to_tok(vT, vTb)
                            interleave(fwd_g(1, blk), proj_conv_g(2048, blk, xT, xTb, tn=qn))

                            def gate2(cj, b_, t0, n, bk, bb, blk=blk):
                                ch8 = blk * 2 + cj
                                a = b_ * L + t0
                                oc = b_ * qn + t0
                                P.add("dve", lambda e: e.scalar_tensor_tensor(out=vT[:, cj, a:a + n], in0=vT[:, cj, a:a + n], scalar=PFB(1, ch8), in1=bk[:, 0:n], op0=ALU.mult, op1=ALU.add),
                                      reads=[vTb[cj], bb, B_pcol], writes=[vTb[cj]])
                                P.add("dve", lambda e: e.tensor_tensor(out=mix[:, ch8, oc:oc + n], in0=vT[:, cj, a:a + n], in1=xT[:, cj, a:a + n], op=ALU.mult),
                                      reads=[vTb[cj], xTb[cj]], writes=mixbufs(ch8, oc, n))
                            inv(qn, gate2)

                with scope() as st:
                    if upto < 3:
                        return
                    P.phase = nm + "_wout"
                    ntt = To // 128
                    racc = sb(st, "racc", [128, ntt, D]); rb = [Buf() for _ in range(ntt)]
                    gbc = sb(st, "gbc", [128, D]); B_g = Buf()
                    lng = sb(st, "lng", [128, D]); lnb = sb(st, "lnb", [128, D]); B_ln = Buf()
                    wbig = Slots(st, "wbig", [128, 16, 512], BF16, 4 if To <= 512 else 3)
                    etmp = Slots(st, "etmp", [128, 512], F32, 3)
                    stats = sb(st, "stats", [128, 4, 6]); mv = sb(st, "mv", [128, 2]); B_st = Buf()
                    for tt in range(ntt):
                        dma("sp", racc[:, tt, :], x_ap[tt * 128:(tt + 1) * 128, :], writes=[rb[tt]])
                    dma("sp", gbc[:], modscr[row:row + 1, 2 * D:3 * D].partition_broadcast(128), reads=[B_modscr], writes=[B_g])
                    dma("sp", lng[:], ln1_g.partition_broadcast(128), writes=[B_ln])
                    dma("sp", lnb[:], ln1_b.partition_broadcast(128), writes=[B_ln])
                    P.add("dve", lambda e: e.tensor_scalar(out=lng[:], in0=lng[:], scalar1=ALPHA, scalar2=None, op0=ALU.mult), reads=[B_ln], writes=[B_ln])
                    P.add("dve", lambda e: e.tensor_scalar(out=lnb[:], in0=lnb[:], scalar1=ALPHA, scalar2=None, op0=ALU.mult), reads=[B_ln], writes=[B_ln])
                    def layer_norm(tt):
                        for q in range(4):
                            P.add("dve", lambda e, q=q: e.bn_stats(out=stats[:, q, :], in_=racc[:, tt, q * 512:(q + 1) * 512]), reads=[rb[tt]], writes=[B_st])
                        P.add("dve", lambda e: e.bn_aggr(out=mv[:], in_=stats[:].rearrange("p a b -> p (a b)")), reads=[B_st], writes=[B_st])
                        P.add("act", lambda e: e.activation(out=mv[:, 1:2], in_=mv[:, 1:2], func=AF.Sqrt, scale=1.0, bias=epsl[:, 0:1]), reads=[B_st, B_eps], writes=[B_st])
                        P.add("dve", lambda e: e.reciprocal(out=mv[:, 1:2], in_=mv[:, 1:2]), reads=[B_st], writes=[B_st])
                        P.add("dve", lambda e: e.tensor_scalar(out=racc[:, tt, :], in0=racc[:, tt, :], scalar1=mv[:, 0:1], scalar2=mv[:, 1:2], op0=ALU.subtract, op1=ALU.mult),
                              reads=[rb[tt], B_st], writes=[rb[tt]])
                        P.add("dve", lambda e: e.tensor_tensor(out=racc[:, tt, :], in0=racc[:, tt, :], in1=lng[:], op=ALU.mult), reads=[rb[tt], B_ln], writes=[rb[tt]])
                        P.add("dve", lambda e: e.tensor_tensor(out=racc[:, tt, :], in0=racc[:, tt, :], in1=lnb[:], op=ALU.add), reads=[rb[tt], B_ln], writes=[rb[tt]])

                    def h2_transposes(tt):
                        for g4 in range(4):
                            bk, bb = psB()
                            for j in range(4):
                                c = g4 * 4 + j
                                P.add("pe", lambda e, bk=bk, j=j, c=c: e.transpose(bk[:, j * 128:(j + 1) * 128], racc[:, tt, c * 128:(c + 1) * 128], ident[:]),
                                      reads=[rb[tt], B_id], writes=[bb])
                            for j in range(4):
                                c = g4 * 4 + j
                                P.add("act", lambda e, bk=bk, j=j, c=c: e.activation(out=mix[:, c, tt * 128:(tt + 1) * 128], in_=bk[:, j * 128:(j + 1) * 128], func=AF.Identity,
                                                                                     scale=MC(row, 3, c), bias=MC(row, 2, c)),
                                      reads=[bb, B_mcol], writes=[mixb[c][tt]])

                    for cc in range(4):
                        wt, wb = wbig.next()
                        dma("pool", wt[:], w_out[:, cc * 512:(cc + 1) * 512].rearrange("(kc p) n -> p kc n", p=128), writes=[wb])
                        cs_ = slice(cc * 512, (cc + 1) * 512)
                        for tt in range(ntt):
                            bk, bb = psA()
                            for kc in range(16):
                                P.add("pe", lambda e, bk=bk, wt=wt, kc=kc, tt=tt: e.matmul(bk[:, :], mix[:, kc, tt * 128:(tt + 1) * 128], wt[:, kc, :], start=(kc == 0), stop=(kc == 15)),
                                      reads=[wb, mixb[kc][tt]], writes=[bb])
                            et, eb = etmp.next()
                            P.add("dve", lambda e, et=et, bk=bk, cs_=cs_: e.tensor_tensor(out=et[:], in0=bk[:, :], in1=gbc[:, cs_], op=ALU.mult), reads=[bb, B_g], writes=[eb])
                            P.add("dve", lambda e, et=et, tt=tt, cs_=cs_: e.scalar_tensor_tensor(out=racc[:, tt, cs_], in0=racc[:, tt, cs_], scalar=ALPHA, in1=et[:], op0=ALU.mult, op1=ALU.add),
                                  reads=[eb, rb[tt]], writes=[rb[tt]])
                            if cc == 3:
                                layer_norm(tt)
                                if tt >= 2:
                                    h2_transposes(tt - 2)
                    for tt in range(max(0, ntt - 2), ntt):
                        h2_transposes(tt)
                    dma("sp", gbc[:], modscr[row:row + 1, 5 * D:6 * D].partition_broadcast(128), reads=[B_modscr, B_g], writes=[B_g])
                    dma("sp", lng[:], ln2_g.partition_broadcast(128), reads=[B_ln], writes=[B_ln])
                    dma("sp", lnb[:], ln2_b.partition_broadcast(128), reads=[B_ln], writes=[B_ln])
                    P.phase = nm + "_mlp"
                    aT = Slots(st, "aT", [128, 4, To], BF16, 2)
                    rl = Slots(st, "rl", [128, 512], F32, 2)
                    def load_w(fg):
                        wu, wub = wbig.next()
                        dma("pool", wu[:], w_up[:, fg * 512:(fg + 1) * 512].rearrange("(kc p) n -> p kc n", p=128), writes=[wub])
                        wd, wdb = wbig.next()
                        wdv = wd[:].rearrange("p a b -> p (a b)").rearrange("p (j n) -> p j n", j=4)
                        dma("pool", wdv, w_down[fg * 512:(fg + 1) * 512, :].rearrange("(j p) n -> p j n", p=128), writes=[wdb])
                        for j in range(4):
                            P.add("pool", lambda e, wdv=wdv, j=j: e.tensor_tensor(out=wdv[:, j, :], in0=wdv[:, j, :], in1=gbc[:], op=ALU.mult), reads=[wdb, B_g], writes=[wdb])
                        return wu, wub, wdv, wdb
                    nxt = load_w(0)
                    gcount = 0
                    for fg in range(16):
                        wu, wub, wdv, wdb = nxt
                        early = To <= 512
                        if early and fg + 1 < 16:
                            nxt = load_w(fg + 1)
                        at, atb = aT.next()
                        for j in range(4):
                            for (t0, n) in own_tiles:
                                bk, bb = psA()
                                for kc in range(16):
                                    P.add("pe", lambda e, bk=bk, wu=wu, kc=kc, j=j, t0=t0, n=n: e.matmul(bk[:, 0:n], wu[:, kc, j * 128:(j + 1) * 128], mix[:, kc, t0:t0 + n], start=(kc == 0), stop=(kc == 15)),
                                          reads=[wub] + mixbufs(kc, t0, n), writes=[bb])
                                rt_, rtb = rl.next()
                                P.add("act", lambda e, bk=bk, rt_=rt_, n=n: e.activation(out=rt_[:, 0:n], in_=bk[:, 0:n], func=AF.Relu), reads=[bb], writes=[rtb])
                                P.add("act", lambda e, rt_=rt_, at=at, j=j, t0=t0, n=n: e.activation(out=at[:, j, t0:t0 + n], in_=rt_[:, 0:n], func=AF.Square),
                                      reads=[rtb], writes=[atb])
                        if (not early) and fg + 1 < 16:
                            nxt = load_w(fg + 1)
                        for tt in range(ntt):
                            for cc in range(4):
                                cs_ = slice(cc * 512, (cc + 1) * 512)
                                bk, bb = psB()
                                for j in range(4):
                                    P.add("pe", lambda e, bk=bk, at=at, wdv=wdv, j=j, tt=tt, cs_=cs_: e.matmul(bk[:, :], at[:, j, tt * 128:(tt + 1) * 128], wdv[:, j, cs_], start=(j == 0), stop=(j == 3)),
                                          reads=[atb, wdb], writes=[bb])
                                P.add("dve", lambda e, bk=bk, tt=tt, cs_=cs_: e.tensor_tensor(out=racc[:, tt, cs_], in0=bk[:, :], in1=racc[:, tt, cs_], op=ALU.add), reads=[bb, rb[tt]], writes=[rb[tt]])
                            if fg == 15:
                                layer_norm(tt)
                                out_ops.append(dma("sp", y_out[tt * 128:(tt + 1) * 128, :], racc[:, tt, :], reads=[rb[tt]]))

        identb = sb(top, "identb", [128, 128], BF16)
        P.add("act", lambda e: e.activation(out=identb[:], in_=ident[:], func=AF.Copy), reads=[B_id], writes=[B_id])

        phase_kf("p", 256)
        with scope() as stm:
            mg = mod_setup(stm)
            phase_kf("s", 2048, extra=mg)
            for _ in mg:
                pass
            mod_finish(stm)
        out_ops.append(dma("sp", dbg[:, 0:128], mcol[:], reads=[B_mcol]))
        out_ops.append(dma("sp", dbg[:, 128:256], pcol[:], reads=[B_pcol]))
        segment("p", 2, 256, 512, x_p, 1, False, False, y_p, nk_p, nv_p)
        segment("s", 1, 2048, 1024, x_s, 0, True, True, y_s, None, None)
        P.final_waits = out_ops
        P.emit()
        global LAST_PE_PHASE
        LAST_PE_PHASE = P.pe_phase
    return nc


_CONST_CACHE = {}


def _consts():
    if not _CONST_CACHE:
        _CONST_CACHE["s"] = _host_consts(2048)
        _CONST_CACHE["p"] = _host_consts(256)
        _CONST_CACHE["rope"] = _rope_tables(2048)
        R = np.zeros((128, 128), np.float32)
        for m in range(64):
            R[m, m + 64] = -1.0
            R[m + 64, m] = 1.0
        _CONST_CACHE["rt"] = np.ascontiguousarray(R.T)
    return _CONST_CACHE


def kernel(x_prompt, x_sample, cache_k, cache_v, c, c_ctx, w_mod, b_mod, w_in, conv_w, conv_b,
           filt_w1, filt_b1, filt_freq1, filt_w2, filt_b2, filt_freq2, filt_w3, filt_b3, filt_bias,
           q_gain, k_gain, w_out, ln1_g, ln1_b, w_up, w_down, ln2_g, ln2_b, _stage=99, _dbg=None):
    f32 = lambda a: np.ascontiguousarray(np.asarray(a, dtype=np.float32))
    cst = _consts()
    C2, S2 = cst["rope"]
    shared = dict(
        w_mod=f32(w_mod[0]), b_mod=f32(b_mod[0]).reshape(1, -1), w_in=f32(w_in[0]),
        conv_b=f32(conv_b[0]), fw1=f32(filt_w1[0]), fb1=f32(filt_b1[0]).reshape(64, 1), ff1=f32(filt_freq1[0]).reshape(64, 1),
        fw2=f32(filt_w2[0]), fb2=f32(filt_b2[0]).reshape(64, 1), ff2=f32(filt_freq2[0]).reshape(64, 1),
        fw3=f32(filt_w3[0]), fb3=f32(filt_b3[0]).reshape(1, -1), filt_bias=f32(filt_bias[0]),
        q_gain=f32(q_gain[0]).reshape(1, 128), k_gain=f32(k_gain[0]).reshape(1, 128),
        w_out=f32(w_out[0]), ln1_g=f32(ln1_g[0]).reshape(1, -1), ln1_b=f32(ln1_b[0]).reshape(1, -1),
        w_up=f32(w_up[0]), w_down=f32(w_down[0]), ln2_g=f32(ln2_g[0]).reshape(1, -1), ln2_b=f32(ln2_b[0]).reshape(1, -1),
        ident=np.eye(128, dtype=np.float32), rt=cst["rt"],
    )
    for nm in ("s", "p"):
        for k in ("FCS", "ICS", "zT", "tcol"):
            shared[f"{k}_{nm}"] = cst[nm][k]
    shared["deltas"] = cst["s"]["deltas"]
    xs = f32(x_sample); xp = f32(x_prompt)
    ck = f32(cache_k); cv = f32(cache_v); cc = f32(c); cctx = f32(c_ctx)
    cw = f32(conv_w[0])
    in_maps = []
    for core in range(8):
        b, half = core // 2, core % 2
        rev = half == 1
        m = dict(shared)
        xb = xs[b]
        xpp = xp[2 * core:2 * core + 2]
        if rev:
            xb = xb[::-1]
            xpp = xpp[:, ::-1]
        m["x_s"] = np.ascontiguousarray(xb)
        m["x_p"] = np.ascontiguousarray(xpp).reshape(512, D)
        m["cache_k"] = np.ascontiguousarray(ck[b, 0].reshape(256, 256))
        m["cache_v"] = np.ascontiguousarray(cv[b, 0].reshape(256, 256))
        m["c2"] = np.ascontiguousarray(np.stack([cc[b], cctx], axis=0))
        m["conv_w"] = np.ascontiguousarray(cw[::-1] if rev else cw)
        w3_ = shared["fw3"]; b3_ = shared["fb3"]
        cols = np.concatenate([np.arange(128) + (o * 2048 + d_ * 1024 + core * 128) for o in range(2) for d_ in range(2)])
        if USE_SHARED_KF:
            m["fw3s"] = np.ascontiguousarray(w3_[:, cols]); m["fb3s"] = np.ascontiguousarray(b3_[:, cols])
            m["decs"] = np.ascontiguousarray(cst["s"]["dec"][:, core * 128:(core + 1) * 128])
            m["dec0s"] = np.ascontiguousarray(cst["s"]["dec0"][:, core * 128:(core + 1) * 128])
        m["sig"] = np.full((128, 1), -1.0 if rev else 1.0, np.float32)
        m["c2tab"] = np.ascontiguousarray(C2[:, ::-1] if rev else C2)
        m["s2tab"] = np.ascontiguousarray(S2[:, ::-1] if rev else S2)
        in_maps.append(m)
    nc = build_program(_stage)
    res = run_bass_kernel_spmd(nc, in_maps, core_ids=list(range(8)))
    y_prompt = np.zeros((16, 256, D), np.float32)
    y_sample = np.zeros((4, 2048, D), np.float32)
    nk = np.zeros((16, 1, 256, 2, 128), np.float32)
    nv = np.zeros((16, 1, 256, 2, 128), np.float32)
    if _dbg is not None:
        _dbg.append([np.asarray(res.results[i]['dbg']) for i in range(8)])
    for core in range(8):
        r = res.results[core]
        b, half = core // 2, core % 2
        ys = np.asarray(r["y_s"]); yp = np.asarray(r["y_p"]).reshape(2, 256, D)
        k_ = np.asarray(r["nk_p"]).reshape(2, 256, 2, 128); v_ = np.asarray(r["nv_p"]).reshape(2, 256, 2, 128)
        if half == 1:
            ys = ys[::-1]; yp = yp[:, ::-1]; k_ = k_[:, ::-1]; v_ = v_[:, ::-1]
            y_sample[b, 1024:] = ys
        else:
            y_sample[b, :1024] = ys
        y_prompt[2 * core:2 * core + 2] = yp
        nk[2 * core:2 * core + 2, 0] = k_
        nv[2 * core:2 * core + 2, 0] = v_
    return (y_prompt, y_sample, nk, nv)
```

```python
import numpy as np
import concourse.bass as bass
import concourse.mybir as mybir

F32 = mybir.dt.float32
BF16 = mybir.dt.bfloat16
AF = mybir.ActivationFunctionType
ALU = mybir.AluOpType
AX = mybir.AxisListType

EPOCH = 8000
NDMASEM = 12
SAME_ENGINE_SYNC = True


class Buf:
    __slots__ = ("lw", "rd", "name", "excl")

    def __init__(self, name="", excl=False):
        self.lw = None
        self.rd = {}
        self.name = name
        self.excl = excl


class Op:
    __slots__ = ("eng", "fn", "deps", "needs_inc", "tok", "dma", "idx", "pre", "solo")

    def __init__(self, eng, fn, dma):
        self.eng = eng
        self.fn = fn
        self.dma = dma
        self.deps = []
        self.needs_inc = dma
        self.tok = None
        self.pre = None
        self.solo = False


class _Rec:
    def __init__(self):
        self.call = None

    def __getattr__(self, name):
        def f(*a, **k):
            self.call = (name, a, k)
            return self
        return f


class Prog:
    ENGS = ("pe", "act", "dve", "pool", "sp")

    def __init__(self, nc):
        self.nc = nc
        self.ops = {e: [] for e in self.ENGS}
        self.final_waits = []
        self.pending = {e: [] for e in self.ENGS}
        self.phase = "init"
        self.pe_phase = []

    def barrier(self):
        deps = []
        for e in self.ENGS:
            last = None
            dmas = []
            for o in reversed(self.ops[e]):
                if o.solo:
                    continue
                if o.dma:
                    if len(dmas) < NDMASEM:
                        dmas.append(o)
                elif last is None:
                    last = o
                if last is not None and len(dmas) >= NDMASEM:
                    break
            if last is not None:
                last.needs_inc = True
                deps.append(last)
            deps.extend(dmas)
        for e in self.ENGS:
            self.pending[e] = list(deps)

    def add(self, eng, fn, reads=(), writes=(), dma=False, solo=False):
        rec = _Rec()
        fn(rec)
        dma = dma or solo
        op = Op(eng, rec.call, dma)
        op.solo = solo
        if eng == "pe":
            self.pe_phase.append(self.phase)
        lst = self.ops[eng]
        op.idx = len(lst)
        deps = {}

        def dep(o, war=False):
            if o is None:
                return
            if o.eng == eng and not o.dma and not dma:
                if eng == "pe" or not SAME_ENGINE_SYNC:
                    return
            deps[id(o)] = o

        for b in reads:
            dep(b.lw)
            if b.excl:
                for o in b.rd.values():
                    if o.eng != eng:
                        dep(o)
        for b in writes:
            dep(b.lw)
            for o in b.rd.values():
                dep(o, war=True)
        for b in reads:
            b.rd[eng if not dma else (eng, "dma", op.idx)] = op
        for b in writes:
            b.lw = op
            b.rd = {}
        if self.pending[eng]:
            for o in self.pending[eng]:
                if not (o.eng == eng and not o.dma and not dma):
                    deps[id(o)] = o
            self.pending[eng] = []
        op.deps = list(deps.values())
        for o in op.deps:
            o.needs_inc = True
        lst.append(op)
        return op

    def emit(self, extra_ctx=()):
        import contextlib

        nc = self.nc
        with contextlib.ExitStack() as st:
            sems = {}
            for e in self.ENGS:
                n_inc = sum(1 for o in self.ops[e] if o.needs_inc and not o.dma)
                n_ep = max(1, (n_inc + EPOCH - 1) // EPOCH)
                sems[e] = [st.enter_context(nc.semaphore(f"s_{e}_{i}")) for i in range(n_ep)]
                n_dma = sum(1 for o in self.ops[e] if o.dma and not o.solo)
                dsem = [st.enter_context(nc.semaphore(f"d_{e}_{i}")) for i in range(min(NDMASEM, n_dma))]
                ci = 0
                di = 0
                for o in self.ops[e]:
                    if o.solo:
                        o.tok = (st.enter_context(nc.semaphore(f"solo_{e}_{o.idx}")), 1)
                    elif o.dma:
                        s = dsem[di % NDMASEM]
                        o.tok = (s, 16 * (di // NDMASEM + 1))
                        if di >= NDMASEM:
                            o.pre = (s, 16 * (di // NDMASEM))
                        di += 1
                    elif o.needs_inc:
                        o.tok = (sems[e][ci // EPOCH], ci % EPOCH + 1)
                        ci += 1
            block = st.enter_context(nc.Block())
            ops = self.ops
            final_waits = self.final_waits

            def gen(e, eng):
                seen = {}

                def wait(tok):
                    s, v = tok
                    k = id(s)
                    if seen.get(k, 0) >= v:
                        return
                    seen[k] = v
                    eng.wait_ge(s, v)

                for o in ops[e]:
                    if o.pre is not None:
                        wait(o.pre)
                    for d in o.deps:
                        wait(d.tok)
                    name, a, k = o.fn
                    ins = getattr(eng, name)(*a, **k)
                    if o.needs_inc:
                        ins.then_inc(o.tok[0], 1 if o.solo else (16 if o.dma else 1))
                if e == "sp":
                    for o in final_waits:
                        wait(o.tok)

            @block.tensor
            def _(eng):
                gen("pe", eng)

            @block.scalar
            def _(eng):
                gen("act", eng)

            @block.vector
            def _(eng):
                gen("dve", eng)

            @block.gpsimd
            def _(eng):
                gen("pool", eng)

            @block.sync
            def _(eng):
                gen("sp", eng)

import math
import contextlib
from concourse.bass_utils import run_bass_kernel_spmd
import ml_dtypes

D = 2048
HY = 1024
HD = 128
O0, O1, O2 = 3072, 4096, 4352
IN_DIM = 4608
DFF = 8192
LN_EPS = 1e-5
RMS_EPS = 1e-6
ALPHA = 2.0 ** 0.25
USE_SHARED_KF = False
MAGIC = 12582912.0
TWO_PI = 2.0 * math.pi


def _host_consts(L):
    N = 2 * L
    t = np.arange(L, dtype=np.float64)
    f = np.arange(L, dtype=np.float64)
    ang = np.pi * np.outer(t, 2 * f + 1) / N
    FC = np.cos(ang)
    FS = -np.sin(ang)
    nt = L // 128
    def slab(M):
        return M.reshape(nt, 128, nt, 128).transpose(2, 1, 0, 3)
    FCS = np.stack([slab(FC), slab(FS)], axis=2)
    IC = (2.0 / N) * FC.T
    IS = (2.0 / N) * FS.T
    ICS = np.stack([IC.reshape(nt, 128, L), IS.reshape(nt, 128, L)], axis=2)
    tl = np.linspace(0.0, 1.0, L, dtype=np.float32)
    w = (2.0 * np.float32(math.pi) * np.arange(L, dtype=np.float32) / np.float32(L)).astype(np.float32)
    bands = np.linspace(1e-4, 15.0, 16, dtype=np.float32)
    arg = (bands[None, :] * w[:, None]).astype(np.float32).astype(np.float64)
    z = np.concatenate([tl[:, None].astype(np.float64), np.cos(arg), -np.sin(arg)], axis=1)
    zT = np.ascontiguousarray(z.T).astype(np.float32)
    min_decay = math.log(1e-2) / 1.5
    max_decay = math.log(1e-2) / 0.3
    deltas = np.abs(np.linspace(min_decay, max_decay, HY, dtype=np.float32)).astype(np.float64)
    dec = np.exp(-tl.astype(np.float64)[:, None] * deltas[None, :]).astype(np.float32)
    dec0 = dec.copy()
    dec0[0, :] = 0.0
    tcol = np.ascontiguousarray((-tl).reshape(nt, 128).T).astype(np.float32)
    return dict(
        tcol=tcol, deltas=deltas.astype(np.float32).reshape(1, HY),
        FCS=np.ascontiguousarray(FCS).astype(ml_dtypes.bfloat16),
        ICS=np.ascontiguousarray(ICS).astype(ml_dtypes.bfloat16),
        zT=zT, dec=dec, dec0=dec0,
    )


def _rope_tables(L):
    rows = L // 64
    row = np.repeat(np.arange(rows, dtype=np.float32), 64)
    col = np.tile(np.arange(64, dtype=np.float32), rows)
    n = HD // 4
    inv = (10000.0 ** (-np.arange(n, dtype=np.float32) / n)).astype(np.float32)
    ang = np.concatenate([row[:, None] * inv, col[:, None] * inv], axis=-1).astype(np.float32)
    c = np.cos(ang.astype(np.float64)).astype(np.float32)
    s = np.sin(ang.astype(np.float64)).astype(np.float32)
    C2 = np.concatenate([c, c], axis=1).T
    S2 = np.concatenate([s, s], axis=1).T
    return np.ascontiguousarray(C2), np.ascontiguousarray(S2)


def build_program(stage=99):
    nc = bass.Bass("TRN2", target_bir_lowering=False)

    def din(name, shape, dt=F32):
        return nc.dram_tensor(name, list(shape), dt, kind="ExternalInput").ap()

    def dout(name, shape):
        return nc.dram_tensor(name, list(shape), F32, kind="ExternalOutput").ap()

    def dscr(name, shape, dt=F32):
        return nc.dram_tensor(name, list(shape), dt, kind="Internal").ap()

    x_s = din("x_s", [2048, D])
    x_p = din("x_p", [512, D])
    cache_k = din("cache_k", [256, 256])
    cache_v = din("cache_v", [256, 256])
    c2 = din("c2", [2, D])
    w_mod = din("w_mod", [D, 6 * D])
    b_mod = din("b_mod", [1, 6 * D])
    w_in = din("w_in", [D, IN_DIM])
    conv_w = din("conv_w", [3, 3072])
    conv_b = din("conv_b", [3072])
    fw1 = din("fw1", [33, 64]); fb1 = din("fb1", [64, 1]); ff1 = din("ff1", [64, 1])
    fw2 = din("fw2", [64, 64]); fb2 = din("fb2", [64, 1]); ff2 = din("ff2", [64, 1])
    fw3 = din("fw3", [64, 4096]); fb3 = din("fb3", [1, 4096])
    filt_bias = din("filt_bias", [2, HY])
    q_gain = din("q_gain", [1, 128]); k_gain = din("k_gain", [1, 128])
    w_out = din("w_out", [D, D])
    ln1_g = din("ln1_g", [1, D]); ln1_b = din("ln1_b", [1, D])
    w_up = din("w_up", [D, DFF]); w_down = din("w_down", [DFF, D])
    ln2_g = din("ln2_g", [1, D]); ln2_b = din("ln2_b", [1, D])
    ident_d = din("ident", [128, 128]); rt_d = din("rt", [128, 128]); sig_d = din("sig", [128, 1])
    c2tab = din("c2tab", [128, 2048]); s2tab = din("s2tab", [128, 2048])
    CS = {}
    for nm, L in (("s", 2048), ("p", 256)):
        nt = L // 128
        CS[nm] = dict(
            FCS=din(f"FCS_{nm}", [nt, 128, 2, nt, 128], BF16),
            ICS=din(f"ICS_{nm}", [nt, 128, 2, L], BF16),
            zT=din(f"zT_{nm}", [33, L]),
            tcol=din(f"tcol_{nm}", [128, nt]),
            K=dscr(f"Kscr_{nm}", [2, 2, nt, 128, HY], BF16),
        )
    modscr = dscr("modscr", [2, 6 * D])
    deltas_d = din("deltas", [1, HY])
    if USE_SHARED_KF:
        fw3s = din("fw3s", [64, 512]); fb3s = din("fb3s", [1, 512])
        decs = din("decs", [2048, 128]); dec0s = din("dec0s", [2048, 128])
    else:
        fw3s = fb3s = decs = dec0s = None
    Kpart = nc.dram_tensor("Kpart", [16, 128, 2, 256], F32, kind="Internal")
    Kall = nc.dram_tensor("Kall", [8, 16, 128, 2, 256], F32, kind="Internal")
    B_Kpart = Buf(); B_Kall = Buf()
    y_s = dout("y_s", [1024, D])
    y_p = dout("y_p", [512, D])
    nk_p = dout("nk_p", [512, 256])
    nv_p = dout("nv_p", [512, 256])
    dbg = dout("dbg", [128, 256])

    P = Prog(nc)
    out_ops = []
    uid = [0]

    def sb(st, name, shape, dt=F32):
        uid[0] += 1
        return st.enter_context(nc.sbuf_tensor(f"{name}_{uid[0]}", list(shape), dt))

    def dma(q, out, in_, reads=(), writes=(), **kw):
        return P.add(q, lambda e: e.dma_start(out=out, in_=in_, **kw), reads=reads, writes=writes, dma=True)

    @contextlib.contextmanager
    def scope():
        with contextlib.ExitStack() as st_:
            try:
                yield st_
            finally:
                P.barrier()

    class Slots:
        def __init__(self, st, name, shape, dt, n):
            self.t = [sb(st, f"{name}{i}", shape, dt) for i in range(n)]
            self.b = [Buf(f"{name}{i}") for i in range(n)]
            self.i = 0

        def next(self):
            k = self.i % len(self.t)
            self.i += 1
            return self.t[k], self.b[k]

    with contextlib.ExitStack() as top:
        banks = [top.enter_context(nc.psum_tensor(f"bank{i}", [128, 512], F32)) for i in range(8)]
        bbuf = [Buf(f"bank{i}", excl=True) for i in range(8)]
        banks_bf = [bk_.bitcast(BF16) for bk_ in banks]
        pc = {"A": 0, "B": 0}

        def psA():
            k = pc["A"] % 4
            pc["A"] += 1
            return banks[k], bbuf[k]

        def psB():
            k = 4 + pc["B"] % 4
            pc["B"] += 1
            return banks[k], bbuf[k]

        ident = sb(top, "ident", [128, 128]); B_id = Buf()
        ones_f = sb(top, "ones_f", [128, 128]); ones_b = sb(top, "ones_b", [128, 128], BF16); B_ones = Buf()
        RT = sb(top, "RT", [128, 128]); B_rt = Buf()
        sig = sb(top, "sig", [128, 1]); B_sig = Buf()
        epsr = sb(top, "epsr", [128, 1]); epsl = sb(top, "epsl", [128, 1]); B_eps = Buf()
        pcol = sb(top, "pcol", [128, 128]); B_pcol = Buf()
        mcol = sb(top, "mcol", [128, 128]); B_mcol = Buf()
        B_modscr = Buf()
        dma("sp", ident[:], ident_d, writes=[B_id])
        dma("sp", RT[:], rt_d, writes=[B_rt])
        dma("sp", sig[:], sig_d, writes=[B_sig])
        P.add("pool", lambda e: e.memset(ones_f[:], 1.0), writes=[B_ones])
        P.add("pool", lambda e: e.memset(ones_b[:], 1.0), writes=[B_ones])
        P.add("pool", lambda e: e.memset(epsr[:], RMS_EPS), writes=[B_eps])
        P.add("pool", lambda e: e.memset(epsl[:], LN_EPS), writes=[B_eps])

        with scope() as st:
            prow = sb(st, "prow", [128, 128]); B_prow = Buf()
            P.add("pool", lambda e: e.memset(prow[:], 0.0), writes=[B_prow])
            dma("sp", prow[0:72, :], conv_w.rearrange("k (c p) -> (k c) p", p=128), writes=[B_prow])
            dma("sp", prow[72:96, :], conv_b.rearrange("(c p) -> c p", p=128), writes=[B_prow])
            dma("sp", prow[96:112, :], filt_bias.rearrange("o (c p) -> (o c) p", p=128), writes=[B_prow])
            dma("sp", prow[112:113, :], q_gain, writes=[B_prow])
            dma("sp", prow[113:114, :], k_gain, writes=[B_prow])
            bk, bb = psA()
            P.add("pe", lambda e: e.transpose(bk[:, 0:128], prow[:], ident[:]), reads=[B_prow, B_id], writes=[bb])
            P.add("dve", lambda e: e.tensor_copy(out=pcol[:], in_=bk[:, 0:128]), reads=[bb], writes=[B_pcol])

        def mod_setup(st):
            cT = sb(st, "cT", [128, 2, 16]); sT = sb(st, "sT", [128, 2, 16], BF16); B_cT = Buf(); B_sT = Buf()
            for r in range(2):
                dma("sp", cT[:, r, :], c2[r].rearrange("(kc p) -> p kc", p=128), writes=[B_cT],
                    allow_slow_non_contiguous=True)
            P.add("act", lambda e: e.activation(out=sT[:], in_=cT[:], func=AF.Silu), reads=[B_cT], writes=[B_sT])
            wm = Slots(st, "wm", [128, 16, 512], BF16, 2)
            bm = Slots(st, "bm", [2, 512], F32, 2)
            mr = Slots(st, "mr", [2, 512], F32, 2)
            return mod_g(sT, B_sT, wm, bm, mr)

        def mod_g(sT, B_sT, wm, bm, mr):
            for cc in range(24):
                wt, wb = wm.next()
                dma("pool", wt[:], w_mod[:, cc * 512:(cc + 1) * 512].rearrange("(kc p) n -> p kc n", p=128), writes=[wb])
                bt, btb = bm.next()
                dma("sp", bt[:], b_mod[:, cc * 512:(cc + 1) * 512].partition_broadcast(2), writes=[btb])
                bk, bb = psA()
                for kc in range(16):
                    P.add("pe", lambda e, bk=bk, wt=wt, kc=kc: e.matmul(bk[0:2, :], sT[:, :, kc], wt[:, kc, :],
                                                                         start=(kc == 0), stop=(kc == 15)),
                          reads=[B_sT, wb], writes=[bb])
                mt, mb = mr.next()
                P.add("dve", lambda e, mt=mt, bk=bk, bt=bt: e.tensor_tensor(out=mt[:], in0=bk[0:2, :], in1=bt[:], op=ALU.add),
                      reads=[bb, btb], writes=[mb])
                dma("act", modscr[:, cc * 512:(cc + 1) * 512], mt[:], reads=[mb], writes=[B_modscr])
                yield

        def mod_finish(st):
            prow2 = sb(st, "prow2", [128, 128]); B_prow2 = Buf()
            for r in range(2):
                for ki, kind in enumerate((0, 1, 3, 4)):
                    dma("sp", prow2[(r * 4 + ki) * 16:(r * 4 + ki + 1) * 16, :],
                        modscr[r, kind * D:(kind + 1) * D].rearrange("(c p) -> c p", p=128),
                        reads=[B_modscr], writes=[B_prow2])
            bk, bb = psA()
            P.add("pe", lambda e, bk=bk: e.transpose(bk[:, 0:128], prow2[:], ident[:]), reads=[B_prow2, B_id], writes=[bb])
            P.add("dve", lambda e, bk=bk: e.tensor_copy(out=mcol[:], in_=bk[:, 0:128]), reads=[bb], writes=[B_mcol])
            for r in range(2):
                a = (r * 4 + 1) * 16
                P.add("dve", lambda e, a=a: e.tensor_scalar(out=mcol[:, a:a + 16], in0=mcol[:, a:a + 16], scalar1=1.0,
                                                            scalar2=None, op0=ALU.add), reads=[B_mcol], writes=[B_mcol])
                a = (r * 4 + 3) * 16
                P.add("dve", lambda e, a=a: e.tensor_scalar(out=mcol[:, a:a + 16], in0=mcol[:, a:a + 16], scalar1=1.0,
                                                            scalar2=1.0 / ALPHA, op0=ALU.add, op1=ALU.mult),
                      reads=[B_mcol], writes=[B_mcol])

        def MC(r, ki, chunk):
            a = (r * 4 + ki) * 16 + chunk
            return mcol[:, a:a + 1]

        def PCW(k, chunk):
            return pcol[:, k * 24 + chunk:k * 24 + chunk + 1]

        def PCB(chunk):
            return pcol[:, 72 + chunk:73 + chunk]

        def PFB(o, ch8):
            return pcol[:, 96 + o * 8 + ch8:97 + o * 8 + ch8]

        QG = pcol[:, 112:113]
        KG = pcol[:, 113:114]

        def phase_kf(nm, L, shared=False, extra=None):
            P.phase = "kf_" + nm
            cs = CS[nm]
            nt = L // 128
            with scope() as st:
                w1 = sb(st, "w1", [33, 64]); w2 = sb(st, "w2", [64, 64]); w3 = sb(st, "w3", [65, 512 if shared else 4096], BF16)
                pf = sb(st, "pf", [64, 8]); B_w = Buf()
                zT = sb(st, "zT", [33, L])
                dma("sp", w1[:], fw1, writes=[B_w]); dma("sp", w2[:], fw2, writes=[B_w])
                dma("pool", w3[0:64, :], fw3s if shared else fw3, writes=[B_w]); dma("pool", w3[64:65, :], fb3s if shared else fb3, writes=[B_w])
                dma("sp", zT[:], cs["zT"], writes=[B_w])
                for i, a in enumerate((fb1, ff1, fb2, ff2)):
                    dma("sp", pf[:, i:i + 1], a, writes=[B_w])
                P.add("dve", lambda e: e.tensor_tensor(out=pf[:, 4:5], in0=pf[:, 0:1], in1=pf[:, 1:2], op=ALU.mult), reads=[B_w], writes=[B_w])
                P.add("dve", lambda e: e.tensor_tensor(out=pf[:, 5:6], in0=pf[:, 2:3], in1=pf[:, 3:4], op=ALU.mult), reads=[B_w], writes=[B_w])
                h1 = sb(st, "h1", [64, L]); h2 = sb(st, "h2", [65, L], BF16); B_h1 = Buf(); B_h2 = Buf()
                arg = sb(st, "arg", [64, 512]); kk = sb(st, "kk", [64, 512]); B_arg = Buf()
                P.add("pool", lambda e: e.memset(h2[64:65, :], 1.0), writes=[B_h2])

                def sin_layer(wt, src, B_src, krows, fcol, fbcol, dst, B_dst):
                    for t0 in range(0, L, 512):
                        n = min(512, L - t0)
                        bk, bb = psA()
                        P.add("pe", lambda e, bk=bk, t0=t0, n=n: e.matmul(bk[0:64, 0:n], wt[0:krows, :], src[0:krows, t0:t0 + n], start=True, stop=True),
                              reads=[B_w, B_src], writes=[bb])
                        P.add("act", lambda e, bk=bk, n=n: e.activation(out=arg[:, 0:n], in_=bk[0:64, 0:n], func=AF.Identity, scale=fcol, bias=fbcol),
                              reads=[bb, B_w], writes=[B_arg])
                        P.add("dve", lambda e, n=n: e.tensor_scalar(out=kk[:, 0:n], in0=arg[:, 0:n], scalar1=1.0 / TWO_PI, scalar2=MAGIC, op0=ALU.mult, op1=ALU.add),
                              reads=[B_arg], writes=[B_arg])
                        P.add("dve", lambda e, n=n: e.tensor_scalar(out=kk[:, 0:n], in0=kk[:, 0:n], scalar1=MAGIC, scalar2=-TWO_PI, op0=ALU.subtract, op1=ALU.mult),
                              reads=[B_arg], writes=[B_arg])
                        P.add("dve", lambda e, n=n: e.tensor_tensor(out=arg[:, 0:n], in0=arg[:, 0:n], in1=kk[:, 0:n], op=ALU.add),
                              reads=[B_arg], writes=[B_arg])
                        P.add("act", lambda e, t0=t0, n=n: e.activation(out=dst[0:64, t0:t0 + n], in_=arg[:, 0:n], func=AF.Sin),
                              reads=[B_arg], writes=[B_dst])

                sin_layer(w1, zT, B_w, 33, pf[:, 1:2], pf[:, 4:5], h1, B_h1)
                sin_layer(w2, h1, B_h1, 64, pf[:, 3:4], pf[:, 5:6], h2, B_h2)
                if shared:
                    hs = sb(st, "hs", [128, nt, 256], BF16); hd = sb(st, "hd", [128, nt, 256], BF16); B_hs = Buf(); B_hd = Buf()
                    dcs = Slots(st, "dcs", [128, 2, 128], F32, 2)
                    tmp = Slots(st, "ktmp", [128, 2, 128], F32, 2)
                    slab = Slots(st, "kslab", [128, 2, nt, 128], BF16, 3)
                    ko = Slots(st, "ko", [128, 2, 256], F32, 2)
                    for tc in range(nt):
                        dt_, db = dcs.next()
                        dma("sp", dt_[:, 0, :], decs[tc * 128:(tc + 1) * 128, :], writes=[db])
                        dma("sp", dt_[:, 1, :], dec0s[tc * 128:(tc + 1) * 128, :], writes=[db])
                        bk, bb = psA()
                        P.add("pe", lambda e, bk=bk, tc=tc: e.matmul(bk[:, :], h2[0:65, tc * 128:(tc + 1) * 128], w3[0:65, :], start=True, stop=True),
                              reads=[B_h2, B_w], writes=[bb])
                        for o in range(2):
                            tt_, tb = tmp.next()
                            for d_ in range(2):
                                ca = (o * 2 + d_) * 128
                                P.add("dve", lambda e, bk=bk, tt_=tt_, dt_=dt_, d_=d_, ca=ca: e.tensor_tensor(out=tt_[:, d_, :], in0=bk[:, ca:ca + 128], in1=dt_[:, d_, :], op=ALU.mult),
                                      reads=[bb, db], writes=[tb])
                            P.add("dve", lambda e, tt_=tt_, tc=tc, o=o: e.tensor_tensor(out=hs[:, tc, o * 128:(o + 1) * 128], in0=tt_[:, 0, :], in1=tt_[:, 1, :], op=ALU.add),
                                  reads=[tb], writes=[B_hs])
                            P.add("dve", lambda e, tt_=tt_, tc=tc, o=o: e.tensor_tensor(out=hd[:, tc, o * 128:(o + 1) * 128], in0=tt_[:, 0, :], in1=tt_[:, 1, :], op=ALU.subtract),
                                  reads=[tb], writes=[B_hd])
                    for fc in range(nt):
                        sl, slb = slab.next()
                        dma("sp", sl[:], cs["FCS"][fc], writes=[slb])
                        b1, bb1 = psA()
                        b2, bb2 = psA()
                        for tc in range(nt):
                            P.add("pe", lambda e, b1=b1, sl=sl, tc=tc: e.matmul(b1[:, 0:256], sl[:, 0, tc, :], hs[:, tc, :], start=(tc == 0), stop=(tc == nt - 1)),
                                  reads=[slb, B_hs], writes=[bb1])
                        for tc in range(nt):
                            P.add("pe", lambda e, b2=b2, sl=sl, tc=tc: e.matmul(b2[:, 0:256], sl[:, 1, tc, :], hd[:, tc, :], start=(tc == 0), stop=(tc == nt - 1)),
                                  reads=[slb, B_hd], writes=[bb2])
                        kt, kb = ko.next()
                        P.add("act", lambda e, kt=kt, b1=b1: e.activation(out=kt[:, 0, :], in_=b1[:, 0:256], func=AF.Copy), reads=[bb1], writes=[kb])
                        P.add("act", lambda e, kt=kt, b2=b2: e.activation(out=kt[:, 1, :], in_=b2[:, 0:256], func=AF.Copy), reads=[bb2], writes=[kb])
                        dma("act", Kpart.ap()[fc], kt[:], reads=[kb], writes=[B_Kpart])
                    P.add("pool", lambda e: e.collective_compute("AllGather", ALU.bypass, replica_groups=[list(range(8))], ins=[Kpart.ap().opt()], outs=[Kall.ap().opt()]),
                          reads=[B_Kpart], writes=[B_Kall], solo=True)
                    return
                dlt = sb(st, "dlt", [128, HY]); tcl = sb(st, "tcl", [128, nt]); B_dlt = Buf()
                dma("sp", dlt[:], deltas_d.partition_broadcast(128), writes=[B_dlt])
                dma("sp", tcl[:], cs["tcol"], writes=[B_dlt])
                hsS = Slots(st, "hs", [128, nt, 512], BF16, 2); hdS = Slots(st, "hd", [128, nt, 512], BF16, 2)
                dcs = Slots(st, "dcs", [128, 1, 512], F32, 2)
                tmp = Slots(st, "ktmp", [128, 2, 512], F32, 2)
                slab = Slots(st, "kslab", [128, 2, nt, 128], BF16, 2)
                ko = Slots(st, "ko", [128, 2, 512], BF16, 2)
                def gen_g(o, cb, hs, B_hs, hd, B_hd):
                    c0 = cb * 512
                    for tc in range(nt):
                        dt_, db = dcs.next()
                        P.add("act", lambda e, dt_=dt_, tc=tc: e.activation(out=dt_[:, 0, :], in_=dlt[:, c0:c0 + 512], func=AF.Exp, scale=tcl[:, tc:tc + 1]),
                              reads=[B_dlt], writes=[db])
                        tt_, tb = tmp.next()
                        for d_ in range(2):
                            bk, bb = psA()
                            col = o * 2048 + d_ * 1024 + c0
                            P.add("pe", lambda e, bk=bk, tc=tc, col=col: e.matmul(bk[:, :], h2[0:65, tc * 128:(tc + 1) * 128], w3[0:65, col:col + 512], start=True, stop=True),
                                  reads=[B_h2, B_w], writes=[bb])
                            P.add("dve", lambda e, bk=bk, tt_=tt_, dt_=dt_, d_=d_: e.tensor_tensor(out=tt_[:, d_, :], in0=bk[:, :], in1=dt_[:, 0, :], op=ALU.mult),
                                  reads=[bb, db], writes=[tb])
                        if tc == 0:
                            P.add("dve", lambda e, tt_=tt_: e.memset(tt_[0:1, 1, :], 0.0), reads=[tb], writes=[tb])
                        P.add("dve", lambda e, tt_=tt_, tc=tc, hs=hs: e.tensor_tensor(out=hs[:, tc, :], in0=tt_[:, 0, :], in1=tt_[:, 1, :], op=ALU.add),
                              reads=[tb], writes=[B_hs])
                        P.add("dve", lambda e, tt_=tt_, tc=tc, hd=hd: e.tensor_tensor(out=hd[:, tc, :], in0=tt_[:, 0, :], in1=tt_[:, 1, :], op=ALU.subtract),
                              reads=[tb], writes=[B_hd])
                        yield

                def dft_g(o, cb, hs, B_hs, hd, B_hd):
                    c0 = cb * 512
                    for fc in range(nt):
                        sl, slb = slab.next()
                        dma("sp", sl[:], cs["FCS"][fc], writes=[slb])
                        b1, bb1 = psB()
                        b2, bb2 = psB()
                        for tc in range(nt):
                            P.add("pe", lambda e, b1=b1, sl=sl, tc=tc: e.matmul(b1[:, :], sl[:, 0, tc, :], hs[:, tc, :], start=(tc == 0), stop=(tc == nt - 1)),
                                  reads=[slb, B_hs], writes=[bb1])
                        for tc in range(nt):
                            P.add("pe", lambda e, b2=b2, sl=sl, tc=tc: e.matmul(b2[:, :], sl[:, 1, tc, :], hd[:, tc, :], start=(tc == 0), stop=(tc == nt - 1)),
                                  reads=[slb, B_hd], writes=[bb2])
                        kt, kb = ko.next()
                        P.add("act", lambda e, kt=kt, b1=b1: e.activation(out=kt[:, 0, :], in_=b1[:, :], func=AF.Copy), reads=[bb1], writes=[kb])
                        P.add("act", lambda e, kt=kt, b2=b2: e.activation(out=kt[:, 1, :], in_=b2[:, :], func=AF.Copy, scale=sig[:, 0:1]), reads=[bb2, B_sig], writes=[kb])
                        dma("act", cs["K"][o, :, fc, :, c0:c0 + 512].rearrange("r p c -> p r c"), kt[:], reads=[kb], writes=[cs["KB"]])
                        yield

                rounds = [0]

                def ilv(*gens):
                    gens = list(gens)
                    while gens:
                        for g in list(gens):
                            try:
                                next(g)
                            except StopIteration:
                                gens.remove(g)
                        rounds[0] += 1
                        if extra is not None and rounds[0] % 3 == 0:
                            try:
                                next(extra)
                            except StopIteration:
                                pass
                prev = None
                for o in range(2):
                    for cb in range(2):
                        hs, B_hs = hsS.next()
                        hd, B_hd = hdS.next()
                        cur = (o, cb, hs, B_hs, hd, B_hd)
                        if prev is None:
                            ilv(gen_g(*cur))
                        else:
                            ilv(dft_g(*prev), gen_g(*cur))
                        prev = cur
                ilv(dft_g(*prev))

        CS["s"]["KB"] = Buf("Ks")
        CS["p"]["KB"] = Buf("Kp")

        def segment(nm, nb, L, To, x_ap, row, has_cache, rope, y_out, nk_out, nv_out, upto=3):
            cs = CS[nm]
            P.phase = nm + "_hT"
            T = nb * L
            nt = L // 128
            koff = 256 if has_cache else 0
            Lk = koff + L
            nkc = Lk // 128
            TL = min(512, L)
            with scope() as sseg:
                mix = sb(sseg, "mix", [128, 16, To], BF16)
                mixb = [[Buf() for _ in range(To // 128)] for _ in range(16)]

                def mixbufs(c, t0, n):
                    return [mixb[c][i] for i in range(t0 // 128, (t0 + n + 127) // 128)]

                with scope() as smx:
                    hT = sb(smx, "hT", [128, 16, T], BF16)
                    hTb = [[Buf() for _ in range(T // 128)] for _ in range(16)]

                    def hbufs(c, t0, n):
                        return [hTb[c][i] for i in range(t0 // 128, (t0 + n + 127) // 128)]

                    with scope() as st:
                        xt = Slots(st, "xt", [128, D], F32, 2)
                        for tt in range(T // 128):
                            xtile, xb = xt.next()
                            dma("sp", xtile[:], x_ap[tt * 128:(tt + 1) * 128, :], writes=[xb])
                            for g4 in range(4):
                                bk, bb = psA()
                                for j in range(4):
                                    c = g4 * 4 + j
                                    P.add("pe", lambda e, bk=bk, j=j, c=c, xtile=xtile: e.transpose(bk[:, j * 128:(j + 1) * 128], xtile[:, c * 128:(c + 1) * 128], ident[:]),
                                          reads=[xb, B_id], writes=[bb])
                                for j in range(4):
                                    c = g4 * 4 + j
                                    if g4 % 2 == 0:
                                        P.add("act", lambda e, bk=bk, j=j, c=c, tt=tt: e.activation(out=hT[:, c, tt * 128:(tt + 1) * 128], in_=bk[:, j * 128:(j + 1) * 128], func=AF.Identity,
                                                                                                    scale=MC(row, 1, c), bias=MC(row, 0, c)),
                                              reads=[bb, B_mcol], writes=[hTb[c][tt]])
                                    else:
                                        P.add("dve", lambda e, bk=bk, j=j, c=c, tt=tt: e.tensor_scalar(out=hT[:, c, tt * 128:(tt + 1) * 128], in0=bk[:, j * 128:(j + 1) * 128],
                                                                                                       scalar1=MC(row, 1, c), scalar2=MC(row, 0, c), op0=ALU.mult, op1=ALU.add),
                                              reads=[bb, B_mcol], writes=[hTb[c][tt]])

                    SUB = 9
                    if SUB < 2:
                        return
                    wsl = Slots(smx, "wsl", [128, 16, 128], BF16, 3)

                    def proj_fm(col, tiles, cb_):
                        for _ in proj_fm_g(col, tiles, cb_):
                            pass

                    def proj_fm_g(col, tiles, cb_):
                        wt, wb = wsl.next()
                        dma("pool", wt[:], w_in[:, col:col + 128].rearrange("(kc p) n -> p kc n", p=128), writes=[wb])
                        for (t0, n) in tiles:
                            bk, bb = psA()
                            for kc in range(16):
                                P.add("pe", lambda e, bk=bk, wt=wt, kc=kc, t0=t0, n=n: e.matmul(bk[:, 0:n], wt[:, kc, :], hT[:, kc, t0:t0 + n], start=(kc == 0), stop=(kc == 15)),
                                      reads=[wb] + hbufs(kc, t0, n), writes=[bb])
                            g_ = cb_(t0, n, bk, bb)
                            if g_ is not None:
                                for pg in list(pend):
                                    try:
                                        next(pg)
                                    except StopIteration:
                                        pend.remove(pg)
                                pend.append(g_)
                            yield

                    pend = []

                    def flush_pend():
                        while pend:
                            for pg in list(pend):
                                try:
                                    next(pg)
                                except StopIteration:
                                    pend.remove(pg)

                    all_tiles = [(b * L + t0, TL) for b in range(nb) for t0 in range(0, L, TL)]
                    own_tiles = [(t0, min(512, To - t0)) for t0 in range(0, To, 512)]

                    P.phase = nm + "_attnproj"
                    with scope() as st:
                        qT = sb(st, "qT", [128, 8, To], BF16); qTb = [Buf() for _ in range(8)]
                        kT = sb(st, "kT", [128, 2, nb, Lk], BF16); kTb = [[Buf() for _ in range(nb)] for _ in range(2)]
                        V = sb(st, "V", [128, nb, nkc, 256], BF16); Vb = [Buf() for _ in range(nb)]
                        if rope:
                            C2 = sb(st, "C2", [128, 2048]); S2 = sb(st, "S2", [128, 2048]); B_rope = Buf()
                            dma("sp", C2[:], c2tab, writes=[B_rope]); dma("sp", S2[:], s2tab, writes=[B_rope])
                        sqS = Slots(st, "sq", [128, 512], F32, 2); rsS = Slots(st, "rs", [128, 512], F32, 2)
                        knS = Slots(st, "kn", [128, 512], F32, 2); t2S = Slots(st, "t2", [128, 512], F32, 2)

                        def post(gcol, out_ap, out_bufs, pos0):
                            def f(t0, n, bk, bb):
                                sq, B_sq = sqS.next(); rs, B_rs = rsS.next(); kn, B_kn = knS.next(); t2, B_t2 = t2S.next()
                                P.add("act", lambda e: e.activation(out=sq[:, 0:n], in_=bk[:, 0:n], func=AF.Square), reads=[bb], writes=[B_sq])
                                b2, bb2 = psB()
                                P.add("pe", lambda e: e.matmul(b2[:, 0:n], ones_f[:], sq[:, 0:n], start=True, stop=True), reads=[B_ones, B_sq], writes=[bb2])
                                P.add("act", lambda e: e.activation(out=rs[:, 0:n], in_=b2[:, 0:n], func=AF.Sqrt, scale=1.0 / HD, bias=epsr[:, 0:1]), reads=[bb2, B_eps], writes=[B_rs])
                                P.add("dve", lambda e: e.reciprocal(out=rs[:, 0:n], in_=rs[:, 0:n]), reads=[B_rs], writes=[B_rs])
                                oa = out_ap(t0, n)
                                ob = out_bufs(t0, n)
                                if not rope:
                                    P.add("dve", lambda e: e.scalar_tensor_tensor(out=oa, in0=bk[:, 0:n], scalar=gcol, in1=rs[:, 0:n], op0=ALU.mult, op1=ALU.mult),
                                          reads=[bb, B_rs, B_pcol], writes=ob)
                                    return
                                P.add("dve", lambda e: e.scalar_tensor_tensor(out=kn[:, 0:n], in0=bk[:, 0:n], scalar=gcol, in1=rs[:, 0:n], op0=ALU.mult, op1=ALU.mult),
                                      reads=[bb, B_rs, B_pcol], writes=[B_kn])
                                yield
                                b3, bb3 = psB()
                                P.add("pe", lambda e: e.matmul(b3[:, 0:n], RT[:], kn[:, 0:n], start=True, stop=True), reads=[B_rt, B_kn], writes=[bb3])
                                p0 = pos0(t0)
                                P.add("dve", lambda e: e.tensor_tensor(out=t2[:, 0:n], in0=b3[:, 0:n], in1=S2[:, p0:p0 + n], op=ALU.mult), reads=[bb3, B_rope], writes=[B_t2])
                                P.add("dve", lambda e: e.tensor_tensor(out=kn[:, 0:n], in0=kn[:, 0:n], in1=C2[:, p0:p0 + n], op=ALU.mult), reads=[B_kn, B_rope], writes=[B_kn])
                                P.add("dve", lambda e: e.tensor_tensor(out=oa, in0=kn[:, 0:n], in1=t2[:, 0:n], op=ALU.add), reads=[B_kn, B_t2], writes=ob)
                                yield
                            return f

                        for hd_ in range(8):
                            proj_fm(O0 + hd_ * 128, own_tiles,
                                    post(QG, lambda t0, n, hd_=hd_: qT[:, hd_, t0:t0 + n], lambda t0, n, hd_=hd_: [qTb[hd_]], lambda t0: t0))
                        for kvh in range(2):
                            proj_fm(O1 + kvh * 128, all_tiles,
                                    post(KG, lambda t0, n, kvh=kvh: kT[:, kvh, t0 // L, koff + (t0 % L):koff + (t0 % L) + n],
                                         lambda t0, n, kvh=kvh: [kTb[kvh][t0 // L]], lambda t0: t0 % L))
                        if SUB < 3:
                            return
                        flush_pend()
                        wv = sb(st, "wv", [128, 16, 256], BF16); B_wv = Buf()
                        dma("pool", wv[:], w_in[:, O2:O2 + 256].rearrange("(kc p) n -> p kc n", p=128), writes=[B_wv])
                        if nv_out is not None:
                            wk = sb(st, "wk", [128, 16, 256], BF16); B_wk = Buf()
                            dma("pool", wk[:], w_in[:, O1:O1 + 256].rearrange("(kc p) n -> p kc n", p=128), writes=[B_wk])
                            kgb = sb(st, "kgb", [128, 128]); B_kgb = Buf()
                            dma("sp", kgb[:], k_gain.partition_broadcast(128), writes=[B_kgb])
                            vo = Slots(st, "vo", [128, 256], F32, 2)
                            kout = Slots(st, "kout", [128, 256], F32, 2)
                            sq2 = sb(st, "sq2", [128, 256]); ss = sb(st, "ss", [128, 2]); B_sq2 = Buf(); B_ss = Buf()
                        for tt in range(T // 128):
                            b_ = (tt * 128) // L
                            tcl = (tt * 128 % L) // 128
                            bk, bb = psA()
                            for kc in range(16):
                                P.add("pe", lambda e, bk=bk, kc=kc, tt=tt: e.matmul(bk[:, 0:256], hT[:, kc, tt * 128:(tt + 1) * 128], wv[:, kc, :], start=(kc == 0), stop=(kc == 15)),
                                      reads=[B_wv, hTb[kc][tt]], writes=[bb])
                            P.add("act", lambda e, bk=bk, b_=b_, tcl=tcl: e.activation(out=V[:, b_, koff // 128 + tcl, :], in_=bk[:, 0:256], func=AF.Copy), reads=[bb], writes=[Vb[b_]])
                            if nv_out is not None:
                                vt, vb = vo.next()
                                P.add("act", lambda e, bk=bk, vt=vt: e.activation(out=vt[:], in_=bk[:, 0:256], func=AF.Copy), reads=[bb], writes=[vb])
                                out_ops.append(dma("sp", nv_out[tt * 128:(tt + 1) * 128, :], vt[:], reads=[vb]))
                                bk2, bb2 = psA()
                                for kc in range(16):
                                    P.add("pe", lambda e, bk2=bk2, kc=kc, tt=tt: e.matmul(bk2[:, 0:256], hT[:, kc, tt * 128:(tt + 1) * 128], wk[:, kc, :], start=(kc == 0), stop=(kc == 15)),
                                          reads=[B_wk, hTb[kc][tt]], writes=[bb2])
                                P.add("act", lambda e, bk2=bk2: e.activation(out=sq2[:], in_=bk2[:, 0:256], func=AF.Square), reads=[bb2], writes=[B_sq2])
                                P.add("dve", lambda e: e.reduce_sum(out=ss[:], in_=sq2[:].rearrange("p (h d) -> p h d", h=2), axis=AX.X), reads=[B_sq2], writes=[B_ss])
                                P.add("act", lambda e: e.activation(out=ss[:], in_=ss[:], func=AF.Sqrt, scale=1.0 / HD, bias=epsr[:, 0:1]), reads=[B_ss, B_eps], writes=[B_ss])
                                P.add("dve", lambda e: e.reciprocal(out=ss[:], in_=ss[:]), reads=[B_ss], writes=[B_ss])
                                kt_, kb_ = kout.next()
                                for h in range(2):
                                    P.add("dve", lambda e, bk2=bk2, kt_=kt_, h=h: e.scalar_tensor_tensor(out=kt_[:, h * 128:(h + 1) * 128], in0=bk2[:, h * 128:(h + 1) * 128], scalar=ss[:, h:h + 1],
                                                                                                        in1=kgb[:], op0=ALU.mult, op1=ALU.mult),
                                          reads=[bb2, B_ss, B_kgb], writes=[kb_])
                                out_ops.append(dma("sp", nk_out[tt * 128:(tt + 1) * 128, :], kt_[:], reads=[kb_]))
                        if has_cache:
                            ck = sb(st, "ck", [128, 2, 256]); B_ck = Buf()
                            dma("sp", ck[:], cache_k.rearrange("(c p) f -> p c f", p=128), writes=[B_ck])
                            dma("pool", V[:, 0, 0:2, :], cache_v.rearrange("(c p) f -> p c f", p=128), writes=[Vb[0]])
                            bk, bb = psA()
                            for c in range(2):
                                for kvh in range(2):
                                    j = c * 2 + kvh
                                    P.add("pe", lambda e, bk=bk, j=j, c=c, kvh=kvh: e.transpose(bk[:, j * 128:(j + 1) * 128], ck[:, c, kvh * 128:(kvh + 1) * 128], ident[:]),
                                          reads=[B_ck, B_id], writes=[bb])
                            for c in range(2):
                                for kvh in range(2):
                                    j = c * 2 + kvh
                                    P.add("act", lambda e, bk=bk, j=j, c=c, kvh=kvh: e.activation(out=kT[:, kvh, 0, c * 128:(c + 1) * 128], in_=bk[:, j * 128:(j + 1) * 128], func=AF.Copy),
                                          reads=[bb], writes=[kTb[kvh][0]])
                        if SUB < 4:
                            return
                        P.phase = nm + "_attncore"
                        PT = Slots(st, "PT", [128, 512], BF16, 4)
                        rc = sb(st, "rc", [128, 512]); B_rc = Buf()
                        qn = To // nb
                        for b_ in range(nb):
                            for hd_ in range(8):
                                kvh = hd_ // 4
                                for q0 in range(0, qn, 512):
                                    n = min(512, qn - q0)
                                    qc = b_ * qn + q0
                                    po, pob = psB()
                                    pm, pmb = psB()
                                    Sd = {}

                                    def issue_S(kc):
                                        bk, bb = psA()
                                        P.add("pe", lambda e: e.matmul(bk[:, 0:n], kT[:, kvh, b_, kc * 128:(kc + 1) * 128], qT[:, hd_, qc:qc + n], start=True, stop=True),
                                              reads=[kTb[kvh][b_], qTb[hd_]], writes=[bb])
                                        pt, ptb = PT.next()
                                        P.add("act", lambda e: e.activation(out=pt[:, 0:n], in_=bk[:, 0:n], func=AF.Exp, scale=HD ** -0.5), reads=[bb], writes=[ptb])
                                        Sd[kc] = (pt, ptb)
                                    for kc in range(min(2, nkc)):
                                        issue_S(kc)
                                    for kc in range(nkc):
                                        if kc + 2 < nkc:
                                            issue_S(kc + 2)
                                        pt, ptb = Sd.pop(kc)
                                        P.add("pe", lambda e: e.matmul(po[:, 0:n], V[:, b_, kc, kvh * 128:(kvh + 1) * 128], pt[:, 0:n], start=(kc == 0), stop=(kc == nkc - 1)),
                                              reads=[Vb[b_], ptb], writes=[pob])
                                        P.add("pe", lambda e: e.matmul(pm[:, 0:n], ones_b[:], pt[:, 0:n], start=(kc == 0), stop=(kc == nkc - 1)),
                                              reads=[B_ones, ptb], writes=[pmb])
                                    P.add("dve", lambda e: e.reciprocal(out=rc[:, 0:n], in_=pm[:, 0:n]), reads=[pmb], writes=[B_rc])
                                    P.add("dve", lambda e: e.tensor_tensor(out=mix[:, 8 + hd_, qc:qc + n], in0=po[:, 0:n], in1=rc[:, 0:n], op=ALU.mult),
                                          reads=[pob, B_rc], writes=mixbufs(8 + hd_, qc, n))

                    with scope() as st:
                        if upto < 2:
                            return
                        P.phase = nm + "_hyena"
                        ncol = nb * 256
                        praw = sb(st, "praw", [128, nb, L + 2]); B_praw = Buf()
                        utmp = sb(st, "utmp", [128, nb, L]); B_ut = Buf()
                        vT = sb(st, "vT", [128, 2, T], BF16); vTb = [Buf(), Buf()]
                        xT = sb(st, "xT", [128, 2, T], BF16); xTb = [Buf(), Buf()]
                        vtok = sb(st, "vtok", [128, nt, ncol], BF16); B_vtok = Buf()
                        Yr = sb(st, "Yr", [128, nt, ncol], BF16); Yi = sb(st, "Yi", [128, nt, ncol], BF16); B_Y = Buf()
                        fsl = Slots(st, "fsl", [128, 2, nt, 128], BF16, 2)
                        isl = Slots(st, "isl", [128, 2, L], BF16, 2)
                        kcs = Slots(st, "kcs", [128, 2, 256], BF16 if not USE_SHARED_KF else F32, 2)
                        ctmp = Slots(st, "ctmp", [128, 4, 256], F32, 1)
                        P.add("pool", lambda e: e.memset(praw[:], 0.0), writes=[B_praw])

                        def proj_conv(colbase, blk, dst, dstb):
                            for _ in proj_conv_g(colbase, blk, dst, dstb):
                                pass

                        def proj_conv_g(colbase, blk, dst, dstb, tn=None):
                            tn = L if tn is None else tn
                            tiles_ = [(b_ * L + t0, min(TL, tn - t0)) for b_ in range(nb) for t0 in range(0, tn, TL)]
                            if tn < L:
                                tiles_ += [(b_ * L + tn, 128) for b_ in range(nb)]
                            for cj in range(2):
                                ch = (colbase // 128) + blk * 2 + cj

                                def cb_(t0, n, bk, bb):
                                    b_ = t0 // L
                                    tl = t0 % L
                                    P.add("act", lambda e: e.activation(out=praw[:, b_, 1 + tl:1 + tl + n], in_=bk[:, 0:n], func=AF.Copy), reads=[bb], writes=[B_praw])
                                yield from proj_fm_g(colbase + blk * 256 + cj * 128, tiles_, cb_)
                                P.add("dve", lambda e, ch=ch: e.tensor_scalar(out=utmp[:, :, 0:tn], in0=praw[:, :, 1:tn + 1], scalar1=PCW(1, ch), scalar2=PCB(ch), op0=ALU.mult, op1=ALU.add),
                                      reads=[B_praw, B_pcol], writes=[B_ut])
                                P.add("dve", lambda e, ch=ch: e.scalar_tensor_tensor(out=utmp[:, :, 0:tn], in0=praw[:, :, 0:tn], scalar=PCW(0, ch), in1=utmp[:, :, 0:tn], op0=ALU.mult, op1=ALU.add),
                                      reads=[B_praw, B_pcol, B_ut], writes=[B_ut])
                                P.add("dve", lambda e, ch=ch, cj=cj: e.scalar_tensor_tensor(out=dst[:, cj, :].rearrange("p (b l) -> p b l", b=nb)[:, :, 0:tn], in0=praw[:, :, 2:tn + 2], scalar=PCW(2, ch), in1=utmp[:, :, 0:tn],
                                                                                            op0=ALU.mult, op1=ALU.add),
                                      reads=[B_praw, B_pcol, B_ut], writes=[dstb[cj]])
                                yield

                        def interleave(*gens):
                            gens = list(gens)
                            while gens:
                                for g in list(gens):
                                    try:
                                        next(g)
                                    except StopIteration:
                                        gens.remove(g)

                        def to_tok(src, srcb):
                            for tc in range(nt):
                                bk, bb = psA()
                                for b_ in range(nb):
                                    for cj in range(2):
                                        j = b_ * 2 + cj
                                        P.add("pe", lambda e, bk=bk, j=j, b_=b_, cj=cj, tc=tc: e.transpose(banks_bf[bbuf.index(bb)][:, j * 128:(j + 1) * 128],
                                                                                                         src[:, cj, b_ * L + tc * 128:b_ * L + (tc + 1) * 128], identb[:]),
                                              reads=[srcb[cj], B_id], writes=[bb])
                                P.add("act", lambda e, bk=bk, tc=tc: e.activation(out=vtok[:, tc, :], in_=banks_bf[bbuf.index(bb)][:, 0:ncol], func=AF.Copy), reads=[bb], writes=[B_vtok])

                        def fwd_g(o, blk):
                            for fc in range(nt):
                                sl, slb = fsl.next()
                                dma("sp", sl[:], cs["FCS"][fc], writes=[slb])
                                kt, kb = kcs.next()
                                if USE_SHARED_KF and nm == "s":
                                    dma("sp", kt[:].rearrange("p i (r c) -> p i r c", r=2),
                                        Kall.ap()[2 * blk:2 * blk + 2, fc, :, :, o * 128:(o + 1) * 128].rearrange("r p i c -> p i r c"), reads=[B_Kall], writes=[kb])
                                    P.add("act", lambda e, kt=kt: e.activation(out=kt[:, 1, :], in_=kt[:, 1, :], func=AF.Copy, scale=sig[:, 0:1]), reads=[kb, B_sig], writes=[kb])
                                else:
                                    dma("sp", kt[:], cs["K"][o, :, fc, :, blk * 256:(blk + 1) * 256].rearrange("r p c -> p r c"), reads=[cs["KB"]], writes=[kb])
                                zr, zrb = psB()
                                zi, zib = psB()
                                for tc in range(nt):
                                    P.add("pe", lambda e, zr=zr, sl=sl, tc=tc: e.matmul(zr[:, 0:ncol], sl[:, 0, tc, :], vtok[:, tc, :], start=(tc == 0), stop=(tc == nt - 1)),
                                          reads=[slb, B_vtok], writes=[zrb])
                                for tc in range(nt):
                                    P.add("pe", lambda e, zi=zi, sl=sl, tc=tc: e.matmul(zi[:, 0:ncol], sl[:, 1, tc, :], vtok[:, tc, :], start=(tc == 0), stop=(tc == nt - 1)),
                                          reads=[slb, B_vtok], writes=[zib])
                                for b_ in range(nb):
                                    ct, ctb = ctmp.next()
                                    cs_ = slice(b_ * 256, (b_ + 1) * 256)
                                    P.add("dve", lambda e, ct=ct, zr=zr, kt=kt, cs_=cs_: e.tensor_tensor(out=ct[:, 0, :], in0=zr[:, cs_], in1=kt[:, 0, :], op=ALU.mult), reads=[zrb, kb], writes=[ctb])
                                    P.add("dve", lambda e, ct=ct, zi=zi, kt=kt, cs_=cs_: e.tensor_tensor(out=ct[:, 1, :], in0=zi[:, cs_], in1=kt[:, 1, :], op=ALU.mult), reads=[zib, kb], writes=[ctb])
                                    P.add("dve", lambda e, ct=ct, zr=zr, kt=kt, cs_=cs_: e.tensor_tensor(out=ct[:, 2, :], in0=zr[:, cs_], in1=kt[:, 1, :], op=ALU.mult), reads=[zrb, kb], writes=[ctb])
                                    P.add("dve", lambda e, ct=ct, zi=zi, kt=kt, cs_=cs_: e.tensor_tensor(out=ct[:, 3, :], in0=zi[:, cs_], in1=kt[:, 0, :], op=ALU.mult), reads=[zib, kb], writes=[ctb])
                                    P.add("dve", lambda e, ct=ct, fc=fc, cs_=cs_: e.tensor_tensor(out=Yr[:, fc, cs_], in0=ct[:, 0, :], in1=ct[:, 1, :], op=ALU.subtract), reads=[ctb], writes=[B_Y])
                                    P.add("dve", lambda e, ct=ct, fc=fc, cs_=cs_: e.tensor_tensor(out=Yi[:, fc, cs_], in0=ct[:, 2, :], in1=ct[:, 3, :], op=ALU.add), reads=[ctb], writes=[B_Y])
                                yield

                        def inv(tn, consume):
                            tiles = [(b_, t0, min(512, tn - t0)) for b_ in range(nb) for t0 in range(0, tn, 512)]
                            outs = [(cj, b_, t0, n) for cj in range(2) for (b_, t0, n) in tiles]
                            assert len(outs) <= 8
                            for fc in range(nt):
                                sl, slb = isl.next()
                                dma("sp", sl[:, :, 0:tn], cs["ICS"][fc, :, :, 0:tn], writes=[slb])
                                for i, (cj, b_, t0, n) in enumerate(outs):
                                    ca = b_ * 256 + cj * 128
                                    P.add("pe", lambda e, i=i, sl=sl, fc=fc, ca=ca, t0=t0, n=n: e.matmul(banks[i][:, 0:n], Yr[:, fc, ca:ca + 128], sl[:, 0, t0:t0 + n], start=(fc == 0), stop=False),
                                          reads=[B_Y, slb], writes=[bbuf[i]])
                                    P.add("pe", lambda e, i=i, sl=sl, fc=fc, ca=ca, t0=t0, n=n: e.matmul(banks[i][:, 0:n], Yi[:, fc, ca:ca + 128], sl[:, 1, t0:t0 + n], start=False, stop=(fc == nt - 1)),
                                          reads=[B_Y, slb], writes=[bbuf[i]])
                            for i, (cj, b_, t0, n) in enumerate(outs):
                                consume(cj, b_, t0, n, banks[i], bbuf[i])

                        qn = To // nb
                        for blk in range(4):
                            proj_conv(0, blk, vT, vTb)
                            to_tok(vT, vTb)
                            interleave(fwd_g(0, blk), proj_conv_g(1024, blk, xT, xTb))

                            def gate1(cj, b_, t0, n, bk, bb, blk=blk):
                                ch8 = blk * 2 + cj
                                a = b_ * L + t0
                                P.add("dve", lambda e: e.scalar_tensor_tensor(out=vT[:, cj, a:a + n], in0=vT[:, cj, a:a + n], scalar=PFB(0, ch8), in1=bk[:, 0:n], op0=ALU.mult, op1=ALU.add),
                                      reads=[vTb[cj], bb, B_pcol], writes=[vTb[cj]])
                                P.add("dve", lambda e: e.tensor_tensor(out=vT[:, cj, a:a + n], in0=vT[:, cj, a:a + n], in1=xT[:, cj, a:a + n], op=ALU.mult),
                                      reads=[vTb[cj], xTb[cj]], writes=[vTb[cj]])
                            inv(L, gate1)
                            to_tok(vT, vTb)
                            interleave(fwd_g(1, blk), proj_conv_g(2048, blk, xT, xTb, tn=qn))

                            def gate2(cj, b_, t0, n, bk, bb, blk=blk):
                                ch8 = blk * 2 + cj
                                a = b_ * L + t0
                                oc = b_ * qn + t0
                                P.add("dve", lambda e: e.scalar_tensor_tensor(out=vT[:, cj, a:a + n], in0=vT[:, cj, a:a + n], scalar=PFB(1, ch8), in1=bk[:, 0:n], op0=ALU.mult, op1=ALU.add),
                                      reads=[vTb[cj], bb, B_pcol], writes=[vTb[cj]])
                                P.add("dve", lambda e: e.tensor_tensor(out=mix[:, ch8, oc:oc + n], in0=vT[:, cj, a:a + n], in1=xT[:, cj, a:a + n], op=ALU.mult),
                                      reads=[vTb[cj], xTb[cj]], writes=mixbufs(ch8, oc, n))
                            inv(qn, gate2)

                with scope() as st:
                    if upto < 3:
                        return
                    P.phase = nm + "_wout"
                    ntt = To // 128
                    racc = sb(st, "racc", [128, ntt, D]); rb = [Buf() for _ in range(ntt)]
                    gbc = sb(st, "gbc", [128, D]); B_g = Buf()
                    lng = sb(st, "lng", [128, D]); lnb = sb(st, "lnb", [128, D]); B_ln = Buf()
                    wbig = Slots(st, "wbig", [128, 16, 512], BF16, 4 if To <= 512 else 3)
                    etmp = Slots(st, "etmp", [128, 512], F32, 3)
                    stats = sb(st, "stats", [128, 4, 6]); mv = sb(st, "mv", [128, 2]); B_st = Buf()
                    for tt in range(ntt):
                        dma("sp", racc[:, tt, :], x_ap[tt * 128:(tt + 1) * 128, :], writes=[rb[tt]])
                    dma("sp", gbc[:], modscr[row:row + 1, 2 * D:3 * D].partition_broadcast(128), reads=[B_modscr], writes=[B_g])
                    dma("sp", lng[:], ln1_g.partition_broadcast(128), writes=[B_ln])
                    dma("sp", lnb[:], ln1_b.partition_broadcast(128), writes=[B_ln])
                    P.add("dve", lambda e: e.tensor_scalar(out=lng[:], in0=lng[:], scalar1=ALPHA, scalar2=None, op0=ALU.mult), reads=[B_ln], writes=[B_ln])
                    P.add("dve", lambda e: e.tensor_scalar(out=lnb[:], in0=lnb[:], scalar1=ALPHA, scalar2=None, op0=ALU.mult), reads=[B_ln], writes=[B_ln])
                    def layer_norm(tt):
                        for q in range(4):
                            P.add("dve", lambda e, q=q: e.bn_stats(out=stats[:, q, :], in_=racc[:, tt, q * 512:(q + 1) * 512]), reads=[rb[tt]], writes=[B_st])
                        P.add("dve", lambda e: e.bn_aggr(out=mv[:], in_=stats[:].rearrange("p a b -> p (a b)")), reads=[B_st], writes=[B_st])
                        P.add("act", lambda e: e.activation(out=mv[:, 1:2], in_=mv[:, 1:2], func=AF.Sqrt, scale=1.0, bias=epsl[:, 0:1]), reads=[B_st, B_eps], writes=[B_st])
                        P.add("dve", lambda e: e.reciprocal(out=mv[:, 1:2], in_=mv[:, 1:2]), reads=[B_st], writes=[B_st])
                        P.add("dve", lambda e: e.tensor_scalar(out=racc[:, tt, :], in0=racc[:, tt, :], scalar1=mv[:, 0:1], scalar2=mv[:, 1:2], op0=ALU.subtract, op1=ALU.mult),
                              reads=[rb[tt], B_st], writes=[rb[tt]])
                        P.add("dve", lambda e: e.tensor_tensor(out=racc[:, tt, :], in0=racc[:, tt, :], in1=lng[:], op=ALU.mult), reads=[rb[tt], B_ln], writes=[rb[tt]])
                        P.add("dve", lambda e: e.tensor_tensor(out=racc[:, tt, :], in0=racc[:, tt, :], in1=lnb[:], op=ALU.add), reads=[rb[tt], B_ln], writes=[rb[tt]])

                    def h2_transposes(tt):
                        for g4 in range(4):
                            bk, bb = psB()
                            for j in range(4):
                                c = g4 * 4 + j
                                P.add("pe", lambda e, bk=bk, j=j, c=c: e.transpose(bk[:, j * 128:(j + 1) * 128], racc[:, tt, c * 128:(c + 1) * 128], ident[:]),
                                      reads=[rb[tt], B_id], writes=[bb])
                            for j in range(4):
                                c = g4 * 4 + j
                                P.add("act", lambda e, bk=bk, j=j, c=c: e.activation(out=mix[:, c, tt * 128:(tt + 1) * 128], in_=bk[:, j * 128:(j + 1) * 128], func=AF.Identity,
                                                                                     scale=MC(row, 3, c), bias=MC(row, 2, c)),
                                      reads=[bb, B_mcol], writes=[mixb[c][tt]])

                    for cc in range(4):
                        wt, wb = wbig.next()
                        dma("pool", wt[:], w_out[:, cc * 512:(cc + 1) * 512].rearrange("(kc p) n -> p kc n", p=128), writes=[wb])
                        cs_ = slice(cc * 512, (cc + 1) * 512)
                        for tt in range(ntt):
                            bk, bb = psA()
                            for kc in range(16):
                                P.add("pe", lambda e, bk=bk, wt=wt, kc=kc, tt=tt: e.matmul(bk[:, :], mix[:, kc, tt * 128:(tt + 1) * 128], wt[:, kc, :], start=(kc == 0), stop=(kc == 15)),
                                      reads=[wb, mixb[kc][tt]], writes=[bb])
                            et, eb = etmp.next()
                            P.add("dve", lambda e, et=et, bk=bk, cs_=cs_: e.tensor_tensor(out=et[:], in0=bk[:, :], in1=gbc[:, cs_], op=ALU.mult), reads=[bb, B_g], writes=[eb])
                            P.add("dve", lambda e, et=et, tt=tt, cs_=cs_: e.scalar_tensor_tensor(out=racc[:, tt, cs_], in0=racc[:, tt, cs_], scalar=ALPHA, in1=et[:], op0=ALU.mult, op1=ALU.add),
                                  reads=[eb, rb[tt]], writes=[rb[tt]])
                            if cc == 3:
                                layer_norm(tt)
                                if tt >= 2:
                                    h2_transposes(tt - 2)
                    for tt in range(max(0, ntt - 2), ntt):
                        h2_transposes(tt)
                    dma("sp", gbc[:], modscr[row:row + 1, 5 * D:6 * D].partition_broadcast(128), reads=[B_modscr, B_g], writes=[B_g])
                    dma("sp", lng[:], ln2_g.partition_broadcast(128), reads=[B_ln], writes=[B_ln])
                    dma("sp", lnb[:], ln2_b.partition_broadcast(128), reads=[B_ln], writes=[B_ln])
                    P.phase = nm + "_mlp"
                    aT = Slots(st, "aT", [128, 4, To], BF16, 2)
                    rl = Slots(st, "rl", [128, 512], F32, 2)
                    fold = To > 512

                    def load_w(fg):
                        wu, wub = wbig.next()
                        dma("pool", wu[:], w_up[:, fg * 512:(fg + 1) * 512].rearrange("(kc p) n -> p kc n", p=128), writes=[wub])
                        wd, wdb = wbig.next()
                        wdv = wd[:].rearrange("p a b -> p (a b)").rearrange("p (j n) -> p j n", j=4)
                        dma("pool", wdv, w_down[fg * 512:(fg + 1) * 512, :].rearrange("(j p) n -> p j n", p=128), writes=[wdb])
                        if fold:
                            for j in range(4):
                                P.add("pool", lambda e, wdv=wdv, j=j: e.tensor_tensor(out=wdv[:, j, :], in0=wdv[:, j, :], in1=gbc[:], op=ALU.mult), reads=[wdb, B_g], writes=[wdb])
                        return wu, wub, wdv, wdb
                    nxt = load_w(0)
                    gcount = 0
                    for fg in range(16):
                        wu, wub, wdv, wdb = nxt
                        early = To <= 512
                        if early and fg + 1 < 16:
                            nxt = load_w(fg + 1)
                        at, atb = aT.next()
                        for j in range(4):
                            for (t0, n) in own_tiles:
                                bk, bb = psA()
                                for kc in range(16):
                                    P.add("pe", lambda e, bk=bk, wu=wu, kc=kc, j=j, t0=t0, n=n: e.matmul(bk[:, 0:n], wu[:, kc, j * 128:(j + 1) * 128], mix[:, kc, t0:t0 + n], start=(kc == 0), stop=(kc == 15)),
                                          reads=[wub] + mixbufs(kc, t0, n), writes=[bb])
                                rt_, rtb = rl.next()
                                P.add("act", lambda e, bk=bk, rt_=rt_, n=n: e.activation(out=rt_[:, 0:n], in_=bk[:, 0:n], func=AF.Relu), reads=[bb], writes=[rtb])
                                P.add("act", lambda e, rt_=rt_, at=at, j=j, t0=t0, n=n: e.activation(out=at[:, j, t0:t0 + n], in_=rt_[:, 0:n], func=AF.Square),
                                      reads=[rtb], writes=[atb])
                        if (not early) and fg + 1 < 16:
                            nxt = load_w(fg + 1)
                        for tt in range(ntt):
                            for cc in range(4):
                                cs_ = slice(cc * 512, (cc + 1) * 512)
                                bk, bb = psB()
                                for j in range(4):
                                    P.add("pe", lambda e, bk=bk, at=at, wdv=wdv, j=j, tt=tt, cs_=cs_: e.matmul(bk[:, :], at[:, j, tt * 128:(tt + 1) * 128], wdv[:, j, cs_], start=(j == 0), stop=(j == 3)),
                                          reads=[atb, wdb], writes=[bb])
                                if fold:
                                    P.add("dve", lambda e, bk=bk, tt=tt, cs_=cs_: e.tensor_tensor(out=racc[:, tt, cs_], in0=bk[:, :], in1=racc[:, tt, cs_], op=ALU.add), reads=[bb, rb[tt]], writes=[rb[tt]])
                                else:
                                    et, eb = etmp.next()
                                    P.add("dve", lambda e, et=et, bk=bk, cs_=cs_: e.tensor_tensor(out=et[:], in0=bk[:, :], in1=gbc[:, cs_], op=ALU.mult), reads=[bb, B_g], writes=[eb])
                                    P.add("dve", lambda e, et=et, tt=tt, cs_=cs_: e.tensor_tensor(out=racc[:, tt, cs_], in0=racc[:, tt, cs_], in1=et[:], op=ALU.add), reads=[eb, rb[tt]], writes=[rb[tt]])
                            if fg == 15:
                                layer_norm(tt)
                                out_ops.append(dma("sp", y_out[tt * 128:(tt + 1) * 128, :], racc[:, tt, :], reads=[rb[tt]]))

        identb = sb(top, "identb", [128, 128], BF16)
        P.add("act", lambda e: e.activation(out=identb[:], in_=ident[:], func=AF.Copy), reads=[B_id], writes=[B_id])

        phase_kf("p", 256)
        with scope() as stm:
            mg = mod_setup(stm)
            phase_kf("s", 2048, extra=mg)
            for _ in mg:
                pass
            mod_finish(stm)
        out_ops.append(dma("sp", dbg[:, 0:128], mcol[:], reads=[B_mcol]))
        out_ops.append(dma("sp", dbg[:, 128:256], pcol[:], reads=[B_pcol]))
        segment("p", 2, 256, 512, x_p, 1, False, False, y_p, nk_p, nv_p)
        segment("s", 1, 2048, 1024, x_s, 0, True, True, y_s, None, None)
        P.final_waits = out_ops
        P.emit()
        global LAST_PE_PHASE
        LAST_PE_PHASE = P.pe_phase
    return nc


_CONST_CACHE = {}


def _consts():
    if not _CONST_CACHE:
        _CONST_CACHE["s"] = _host_consts(2048)
        _CONST_CACHE["p"] = _host_consts(256)
        _CONST_CACHE["rope"] = _rope_tables(2048)
        R = np.zeros((128, 128), np.float32)
        for m in range(64):
            R[m, m + 64] = -1.0
            R[m + 64, m] = 1.0
        _CONST_CACHE["rt"] = np.ascontiguousarray(R.T)
    return _CONST_CACHE


def kernel(x_prompt, x_sample, cache_k, cache_v, c, c_ctx, w_mod, b_mod, w_in, conv_w, conv_b,
           filt_w1, filt_b1, filt_freq1, filt_w2, filt_b2, filt_freq2, filt_w3, filt_b3, filt_bias,
           q_gain, k_gain, w_out, ln1_g, ln1_b, w_up, w_down, ln2_g, ln2_b, _stage=99, _dbg=None):
    f32 = lambda a: np.ascontiguousarray(np.asarray(a, dtype=np.float32))
    cst = _consts()
    C2, S2 = cst["rope"]
    shared = dict(
        w_mod=f32(w_mod[0]), b_mod=f32(b_mod[0]).reshape(1, -1), w_in=f32(w_in[0]),
        conv_b=f32(conv_b[0]), fw1=f32(filt_w1[0]), fb1=f32(filt_b1[0]).reshape(64, 1), ff1=f32(filt_freq1[0]).reshape(64, 1),
        fw2=f32(filt_w2[0]), fb2=f32(filt_b2[0]).reshape(64, 1), ff2=f32(filt_freq2[0]).reshape(64, 1),
        fw3=f32(filt_w3[0]), fb3=f32(filt_b3[0]).reshape(1, -1), filt_bias=f32(filt_bias[0]),
        q_gain=f32(q_gain[0]).reshape(1, 128), k_gain=f32(k_gain[0]).reshape(1, 128),
        w_out=f32(w_out[0]), ln1_g=f32(ln1_g[0]).reshape(1, -1), ln1_b=f32(ln1_b[0]).reshape(1, -1),
        w_up=f32(w_up[0]), w_down=f32(w_down[0]), ln2_g=f32(ln2_g[0]).reshape(1, -1), ln2_b=f32(ln2_b[0]).reshape(1, -1),
        ident=np.eye(128, dtype=np.float32), rt=cst["rt"],
    )
    for nm in ("s", "p"):
        for k in ("FCS", "ICS", "zT", "tcol"):
            shared[f"{k}_{nm}"] = cst[nm][k]
    shared["deltas"] = cst["s"]["deltas"]
    xs = f32(x_sample); xp = f32(x_prompt)
    ck = f32(cache_k); cv = f32(cache_v); cc = f32(c); cctx = f32(c_ctx)
    cw = f32(conv_w[0])
    in_maps = []
    for core in range(8):
        b, half = core // 2, core % 2
        rev = half == 1
        m = dict(shared)
        xb = xs[b]
        xpp = xp[2 * core:2 * core + 2]
        if rev:
            xb = xb[::-1]
            xpp = xpp[:, ::-1]
        m["x_s"] = np.ascontiguousarray(xb)
        m["x_p"] = np.ascontiguousarray(xpp).reshape(512, D)
        m["cache_k"] = np.ascontiguousarray(ck[b, 0].reshape(256, 256))
        m["cache_v"] = np.ascontiguousarray(cv[b, 0].reshape(256, 256))
        m["c2"] = np.ascontiguousarray(np.stack([cc[b], cctx], axis=0))
        m["conv_w"] = np.ascontiguousarray(cw[::-1] if rev else cw)
        w3_ = shared["fw3"]; b3_ = shared["fb3"]
        cols = np.concatenate([np.arange(128) + (o * 2048 + d_ * 1024 + core * 128) for o in range(2) for d_ in range(2)])
        if USE_SHARED_KF:
            m["fw3s"] = np.ascontiguousarray(w3_[:, cols]); m["fb3s"] = np.ascontiguousarray(b3_[:, cols])
            m["decs"] = np.ascontiguousarray(cst["s"]["dec"][:, core * 128:(core + 1) * 128])
            m["dec0s"] = np.ascontiguousarray(cst["s"]["dec0"][:, core * 128:(core + 1) * 128])
        m["sig"] = np.full((128, 1), -1.0 if rev else 1.0, np.float32)
        m["c2tab"] = np.ascontiguousarray(C2[:, ::-1] if rev else C2)
        m["s2tab"] = np.ascontiguousarray(S2[:, ::-1] if rev else S2)
        in_maps.append(m)
    nc = build_program(_stage)
    res = run_bass_kernel_spmd(nc, in_maps, core_ids=list(range(8)))
    y_prompt = np.zeros((16, 256, D), np.float32)
    y_sample = np.zeros((4, 2048, D), np.float32)
    nk = np.zeros((16, 1, 256, 2, 128), np.float32)
    nv = np.zeros((16, 1, 256, 2, 128), np.float32)
    if _dbg is not None:
        _dbg.append([np.asarray(res.results[i]['dbg']) for i in range(8)])
    for core in range(8):
        r = res.results[core]
        b, half = core // 2, core % 2
        ys = np.asarray(r["y_s"]); yp = np.asarray(r["y_p"]).reshape(2, 256, D)
        k_ = np.asarray(r["nk_p"]).reshape(2, 256, 2, 128); v_ = np.asarray(r["nv_p"]).reshape(2, 256, 2, 128)
        if half == 1:
            ys = ys[::-1]; yp = yp[:, ::-1]; k_ = k_[:, ::-1]; v_ = v_[:, ::-1]
            y_sample[b, 1024:] = ys
        else:
            y_sample[b, :1024] = ys
        y_prompt[2 * core:2 * core + 2] = yp
        nk[2 * core:2 * core + 2, 0] = k_
        nv[2 * core:2 * core + 2, 0] = v_
    return (y_prompt, y_sample, nk, nv)
```
